# Optimizing a Trainium2 kernel written in Bass

```python
import math
import jax, jax.numpy as jnp
from jax import lax
import numpy as np

D_MODEL = 2048
BATCH = 4
SEQ = 2048
DEPTH = 1

N_META = 16
ATTN_WIDTH = D_MODEL // 2
POOL_WIDTH = D_MODEL - ATTN_WIDTH
N_HEADS = 8
HEAD_DIM = ATTN_WIDTH // N_HEADS
KV_RANK = D_MODEL // 8
IDX_HEADS = 16
IDX_DIM = 64
TOPK_MAX = 256
POOL_WINDOWS = (2, 4, 8, 16)
POOL_GROUP = POOL_WIDTH // len(POOL_WINDOWS)
D_FF = ((8 * D_MODEL // 3 + 255) // 256) * 256
CONV_WIDTH = 3
REL_BUCKETS = 32
REL_MAX_DIST = 128
Q_BLOCK = 128
ALPHA = (2.0 * DEPTH) ** 0.25
BETA = (8.0 * DEPTH) ** -0.25
LN_EPS = 1e-5
NEG_INF = -1e30
PROJ_SIZES = (ATTN_WIDTH, KV_RANK, IDX_HEADS * IDX_DIM, IDX_DIM, IDX_HEADS, POOL_WIDTH)
PROJ_COLS = sum(PROJ_SIZES)

kernel_name = "hybrid_dsa_pool_deepnorm_layer"


def layer_norm(x, g, b):
    xf = x.astype(jnp.float32)
    mu = jnp.mean(xf, axis=-1, keepdims=True)
    var = jnp.mean(jnp.square(xf - mu), axis=-1, keepdims=True)
    return ((xf - mu) * lax.rsqrt(var + LN_EPS) * g.astype(jnp.float32) + b.astype(jnp.float32)).astype(x.dtype)


def rms_norm(x, g):
    xf = x.astype(jnp.float32)
    return (xf * lax.rsqrt(jnp.mean(jnp.square(xf), axis=-1, keepdims=True) + LN_EPS) * g.astype(jnp.float32)).astype(x.dtype)


def split_columns(p):
    offs = np.cumsum(PROJ_SIZES)[:-1].tolist()
    return jnp.split(p, offs, axis=-1)


def t5_bucket(dist):
    max_exact = REL_BUCKETS // 2
    d_f = jnp.maximum(dist, 1).astype(jnp.float32)
    large = max_exact + (jnp.log(d_f / max_exact) / math.log(REL_MAX_DIST / max_exact)
                         * (REL_BUCKETS - max_exact)).astype(jnp.int32)
    large = jnp.minimum(large, REL_BUCKETS - 1)
    return jnp.where(dist < max_exact, dist, large)


def dsa_attention(q_abs, c_kv, q_idx, k_idx, w_idx, rel_bias, top_k):
    B, T = c_kv.shape[0], c_kv.shape[1]
    n_blocks = -(-T // Q_BLOCK)
    t_pad = n_blocks * Q_BLOCK - T

    def to_blocks(a):
        a = jnp.pad(a, [(0, 0), (0, t_pad)] + [(0, 0)] * (a.ndim - 2))
        return jnp.moveaxis(a.reshape((B, n_blocks, Q_BLOCK) + a.shape[2:]), 1, 0)

    key_pos = jnp.arange(T, dtype=jnp.int32)

    def block(args):
        qa, qi, wi, blk = args
        t = blk * Q_BLOCK + jnp.arange(Q_BLOCK, dtype=jnp.int32)
        dots = jnp.einsum('bqhd,bsd->bqhs', qi, k_idx)
        score = jnp.einsum('bqhs,bqh->bqs', jax.nn.relu(dots), wi).astype(jnp.float32)
        causal = key_pos[None, :] <= t[:, None]
        score = jnp.where(causal[None], score, NEG_INF)
        _, sel = lax.top_k(score, top_k)
        c_sel = jax.vmap(lambda c, i: c[i])(c_kv, sel)
        logits = jnp.einsum('bqhc,bqkc->bhqk', qa, c_sel).astype(jnp.float32) * (HEAD_DIM ** -0.5)
        dist = t[None, :, None] - sel
        bias = rel_bias[t5_bucket(jnp.maximum(dist, 0))].astype(jnp.float32)
        logits = logits + jnp.moveaxis(bias, -1, 1)
        logits = jnp.where((dist >= 0)[:, None], logits, NEG_INF)
        p = jax.nn.softmax(logits, axis=-1).astype(c_sel.dtype)
        return jnp.einsum('bhqk,bqkc->bqhc', p, c_sel)

    out = lax.map(block, (to_blocks(q_abs), to_blocks(q_idx), to_blocks(w_idx),
                          jnp.arange(n_blocks, dtype=jnp.int32)))
    out = jnp.moveaxis(out, 0, 1).reshape((B, n_blocks * Q_BLOCK) + out.shape[3:])
    return out[:, :T]


def multiscale_pool(u, w_pool, scale):
    B, T, C = u.shape
    cs = jnp.cumsum(u.astype(jnp.float32), axis=1)
    cp = jnp.concatenate([jnp.zeros((B, 1, C), jnp.float32), cs], axis=1)
    pos = jnp.arange(T, dtype=jnp.int32)
    outs = []
    for g, w in enumerate(POOL_WINDOWS):
        lo = g * POOL_GROUP
        cg = cp[..., lo:lo + POOL_GROUP]
        start = jnp.maximum(pos + 1 - w, 0)
        win_sum = cg[:, 1:] - jnp.take(cg, start, axis=1)
        count = jnp.minimum(pos + 1, w).astype(jnp.float32)[None, :, None]
        diff = (win_sum / count).astype(u.dtype) - u[..., lo:lo + POOL_GROUP]
        outs.append(diff @ w_pool[g])
    return jnp.concatenate(outs, axis=-1) * scale


def conv_gated_ffn(h, w_up, conv_w, conv_b, w_down):
    T = h.shape[1]
    z = h @ w_up
    zp = jnp.pad(z, ((0, 0), (CONV_WIDTH - 1, 0), (0, 0)))
    z = sum(zp[:, j:j + T] * conv_w[j] for j in range(CONV_WIDTH)) + conv_b
    a, g = jnp.split(z, 2, axis=-1)
    return (jax.nn.gelu(a) * g) @ w_down


def setup_inputs(seed: int = 0) -> dict:
    key = jax.random.key(seed)
    ks = jax.random.split(key, 20)
    f32 = jnp.float32
    nrm = lambda k, s, sc: jax.random.normal(k, s, f32) * sc
    L = DEPTH
    return {
        "x": nrm(ks[0], (BATCH, SEQ, D_MODEL), 1.0),
        "meta": nrm(ks[1], (N_META, D_MODEL), 1.0),
        "rel_bias": nrm(ks[2], (REL_BUCKETS, N_HEADS), 0.5),
        "w_in": nrm(ks[3], (L, D_MODEL, PROJ_COLS), D_MODEL ** -0.5),
        "kv_norm_g": 1.0 + nrm(ks[4], (L, KV_RANK), 0.02),
        "w_uk": nrm(ks[5], (L, KV_RANK, N_HEADS, HEAD_DIM), KV_RANK ** -0.5),
        "w_uv": nrm(ks[6], (L, KV_RANK, N_HEADS, HEAD_DIM), BETA * KV_RANK ** -0.5),
        "w_pool": nrm(ks[7], (L, len(POOL_WINDOWS), POOL_GROUP, POOL_GROUP), BETA * POOL_GROUP ** -0.5),
        "pool_scale": 1.0 + nrm(ks[8], (L, POOL_WIDTH), 0.02),
        "w_o": nrm(ks[9], (L, D_MODEL, D_MODEL), BETA * D_MODEL ** -0.5),
        "ln1_g": 1.0 + nrm(ks[10], (L, D_MODEL), 0.02),
        "ln1_b": nrm(ks[11], (L, D_MODEL), 0.02),
        "w_up": nrm(ks[12], (L, D_MODEL, 2 * D_FF), D_MODEL ** -0.5),
        "conv_w": nrm(ks[13], (L, CONV_WIDTH, 2 * D_FF), CONV_WIDTH ** -0.5),
        "conv_b": nrm(ks[14], (L, 2 * D_FF), 0.02),
        "w_down": nrm(ks[15], (L, D_FF, D_MODEL), BETA * D_FF ** -0.5),
        "ln2_g": 1.0 + nrm(ks[16], (L, D_MODEL), 0.02),
        "ln2_b": nrm(ks[17], (L, D_MODEL), 0.02),
    }


def reference(x, meta, rel_bias, w_in, kv_norm_g, w_uk, w_uv, w_pool, pool_scale, w_o,
              ln1_g, ln1_b, w_up, conv_w, conv_b, w_down, ln2_g, ln2_b):
    B, S, D = x.shape
    top_k = min(TOPK_MAX, S // 4)
    h = jnp.concatenate([jnp.broadcast_to(meta[None].astype(x.dtype), (B, N_META, D)), x], axis=1)
    T = h.shape[1]
    for l in range(DEPTH):
        q, c_kv, q_idx, k_idx, w_idx, u = split_columns(h @ w_in[l])
        q = q.reshape(B, T, N_HEADS, HEAD_DIM)
        c_kv = rms_norm(c_kv, kv_norm_g[l])
        q_abs = jnp.einsum('bthd,chd->bthc', q, w_uk[l])
        q_idx = q_idx.reshape(B, T, IDX_HEADS, IDX_DIM) * (IDX_DIM ** -0.5)
        w_idx = w_idx * (IDX_HEADS ** -0.5)
        o_lat = dsa_attention(q_abs, c_kv, q_idx, k_idx, w_idx, rel_bias, top_k)
        attn_out = jnp.einsum('bthc,chd->bthd', o_lat, w_uv[l]).reshape(B, T, ATTN_WIDTH)
        pool_out = multiscale_pool(u, w_pool[l], pool_scale[l])
        mix = jnp.concatenate([attn_out, pool_out], axis=-1) @ w_o[l]
        h = layer_norm(ALPHA * h + mix, ln1_g[l], ln1_b[l])
        f = conv_gated_ffn(h, w_up[l], conv_w[l], conv_b[l], w_down[l])
        h = layer_norm(ALPHA * h + f, ln2_g[l], ln2_b[l])
    return h[:, N_META:]
```

```python
import math
import os
import numpy as np
import concourse.bass as bass
import concourse.mybir as mybir
from concourse.bass_utils import run_bass_kernel_spmd
from contextlib import ExitStack

F32 = mybir.dt.float32
F32R = mybir.dt.float32r
U8 = mybir.dt.uint8
AF = mybir.ActivationFunctionType
ALU = mybir.AluOpType
AX = mybir.AxisListType

D = 2048
NT = 114
TPG = 3
G = NT * TPG
NGR = 3
NQT = 9
KC = 19
NK = KC * NT
KOFF = 10 * NT
HALO = 16
GX = G + HALO
DFF = 5632
NFF = 44
ALPHA = 2.0 ** 0.25
EPS = 1e-5
NIT = 18
MBIG = 30000.0
BF16 = mybir.dt.bfloat16
NEG = -1.0e30
C_Q, C_KV, C_QI, C_KI, C_WI, C_U = 0, 1024, 1280, 2304, 2368, 2384


class Prog:
    ENG = ("pe", "act", "dve", "pool", "sp")

    def __init__(self, nc, es):
        self.nc = nc
        self.es = es
        self.streams = {e: [] for e in self.ENG}
        self.count = {e: 0 for e in self.ENG}
        self.sems = {}
        for e in self.ENG:
            self.sems[("eng", e)] = es.enter_context(nc.semaphore("sem_" + e))
        self.dma_count = {}
        self.last_w = {}
        self.readers = {}
        self.waited = {e: {} for e in self.ENG}

    def _dma_sem(self, slot):
        k = ("dma", slot)
        if k not in self.sems:
            self.sems[k] = self.es.enter_context(self.nc.semaphore("dsem%d" % len(self.sems)))
            self.dma_count[slot] = 0
        return k

    def retire(self, old, new):
        toks = []
        for n in old:
            t = self.last_w.get(n)
            if t is not None:
                toks.append(t)
            toks.extend(self.readers.get(n, ()))
        for n in new:
            self.readers[n] = list(toks) + list(self.readers.get(n, ()))

    def op(self, eng, fn, reads=(), writes=(), dma_slot=None, n_dma=1):
        writes = list(writes) + [r for r in reads if r.startswith("ps")]
        reads = [r for r in reads if not r.startswith("ps")]
        deps = {}

        def add(tok):
            if tok is None:
                return
            k, v = tok
            if deps.get(k, 0) < v:
                deps[k] = v
        for r in reads:
            add(self.last_w.get(r))
        for w in writes:
            add(self.last_w.get(w))
            for t in self.readers.get(w, ()):
                add(t)
        waits = []
        for k, v in deps.items():
            if k == ("eng", "pe") and eng == "pe":
                continue
            if self.waited[eng].get(k, 0) >= v:
                continue
            self.waited[eng][k] = v
            waits.append((k, v))
        if dma_slot is not None:
            k = self._dma_sem(dma_slot)
            self.dma_count[dma_slot] += 16 * n_dma
            tok = (k, self.dma_count[dma_slot])
            inc = 16
        else:
            self.count[eng] += 1
            tok = (("eng", eng), self.count[eng])
            inc = 1
        self.streams[eng].append((waits, fn, tok[0], inc))
        for r in reads:
            self.readers.setdefault(r, []).append(tok)
        for w in writes:
            self.last_w[w] = tok
            self.readers[w] = []
        return tok

    def final_wait(self, eng, toks):
        self.streams[eng].append(([(k, v) for (k, v) in toks], None, None, 0))

    def emit(self):
        nc = self.nc
        with nc.Block() as block:
            def run(ename):
                def body(e):
                    for waits, fn, semk, inc in self.streams[ename]:
                        for k, v in waits:
                            e.wait_ge(self.sems[k], v)
                        if fn is None:
                            continue
                        res = fn(e)
                        if isinstance(res, (list, tuple)):
                            for r in res:
                                r.then_inc(self.sems[semk], inc)
                        else:
                            res.then_inc(self.sems[semk], inc)
                return body
            block.tensor(run("pe"))
            block.scalar(run("act"))
            block.vector(run("dve"))
            block.gpsimd(run("pool"))
            block.sync(run("sp"))


class _Done(Exception):
    pass


def build_nc(stage=None):
    nc = bass.Bass("TRN2", target_bir_lowering=False)
    dumps = []

    def din(name, shape, dt=F32):
        return nc.dram_tensor(name, list(shape), dt, kind="ExternalInput").ap()
    xk = din("xk", [NK, D])
    win_m = din("win_m", [24, 128, 16 * 128])
    win_k = din("win_k", [128, 16 * 320])
    win_w = din("win_w", [128, 16 * 16])
    wukT = din("wukT", [128, 8 * 256])
    wuv = din("wuv", [128, 2 * 8 * 128])
    wpool = din("wpool", [128, 4 * 2 * 256])
    wo = din("wo", [16, 128, 16 * 128])
    wup = din("wup", [88, 128, 16 * 128])
    wdn = din("wdn", [16, 4, 128, 11 * 128])
    convp = din("convp", [128, 88 * 4])
    pscale_d = din("pscale", [128, 8])
    gk_d = din("gk", [1, 256])
    ln_d = din("lnp", [4, D])
    idn_d = din("idn", [128, 128])
    ones_d = din("ones", [128, 128])
    bd_d = din("bd", [NT, 8 * NT])
    bp_d = din("bp", [NT, 8 * NT])
    b31_d = din("b31", [NT, 8 * NT])
    tri01_d = din("tri01", [NT, NT])
    trineg_d = din("trineg", [NT, NT])
    kflag_d = din("kflag", [1, KOFF], U8)
    fix_d = din("fixc", [128, 1])
    ln1c_d = din("ln1c", [128, 32])
    out = nc.dram_tensor("out", [1024, D], F32, kind="ExternalOutput").ap()

    with ExitStack() as es:
      try:
          P = Prog(nc, es)

          def sb(name, shape, dt=F32):
              return es.enter_context(nc.sbuf_tensor("s_" + name, list(shape), dt))

          def psum(name):
              return es.enter_context(nc.psum_tensor("p_" + name, [128, 512], F32))

          def fr(ap):
              return ap.bitcast(F32)

          def stop(sid, items):
              if stage != sid:
                  return
              nx = int(os.environ.get("EXTRA_PE", "0"))
              if nx:
                  def fx(e):
                      last = None
                      for _ in range(nx):
                          last = e.matmul(PS[7][:, 0:128], ones[:, :], ones[:, :], start=True, stop=True)
                      return last
                  P.op("pe", fx, reads=["ones"], writes=["ps7"])
              toks = []
              for name, ap, res in items:
                  shp = list(ap.shape)
                  d = nc.dram_tensor("dbg_" + name, shp, F32, kind="ExternalOutput").ap()
                  dumps.append("dbg_" + name)
                  toks.append(P.op("sp", lambda e, d=d, ap=ap: e.dma_start(out=d, in_=ap), reads=[res], dma_slot="dbg_" + name))
              P.final_wait("sp", toks)
              P.emit()
              raise _Done()

          ckvT = sb("ckvT", [128, 2, NK], F32R)
          ckv = sb("ckv", [128, KC, 256], F32R)
          kT2 = sb("kT2", [128, NK], F32R)
          wsm = sb("wsm", [128, 2048], F32R)
          wwi = sb("wwi", [128, 16, 16], F32R)
          ones = sb("ones", [128, 128], F32R)
          cat = sb("cat", [128, 16, G], F32R)
          wb = sb("wb", [128, 2, 2048], F32R)
          REG = sb("REG", [128, NFF * G], F32R)
          dT = sb("dT", [128, 2, G], F32R)
          qtmp = sb("qtmp", [128, 2, G], F32R)
          em = sb("em", [128, 2, 456], F32R)
          rl = sb("rl", [128, 2, 456], F32R)
          idnR = sb("idnR", [128, 128], F32R)
          idnB = sb("idnB", [128, 128], BF16)
          idn = sb("idn", [128, 128])
          Bd = sb("Bd", [128, 8, NT], F32R)
          Bp = sb("Bp", [128, 8, NT], F32R)
          kflag = sb("kflag", [128, KOFF], U8)
          trineg = sb("trineg", [128, NT])
          pw = sb("pw", [128, NIT + 2])
          convw = sb("convw", [128, 88, 4])
          pscale = sb("pscale", [128, 8])
          gkb = sb("gkb", [128, 256])
          halo = sb("halo", [128, 88, 2])
          fixc = sb("fixc", [128, 1])
          hmask = sb("hmask", [128, 2])
          junk = sb("junk", [128, NK], U8)
          widx = sb("widx", [128, TPG, 16])
          st = sb("st", [128, 64])
          st2 = sb("st2", [128, 64])
          ln1c = sb("ln1c", [128, 32])
          wcols = sb("wcols", [128, NIT + 2])
          SCR = sb("SCR", [128, 6144])
          PS = [psum("ps%d" % i) for i in range(8)]

          xTe = REG[:, 0:16 * GX].rearrange("p (k t) -> p k t", k=16)
          olat = REG[:, 0:16 * G].rearrange("p (c h t) -> p c h t", c=2, h=8)
          mixT = REG[:, 0:16 * G].rearrange("p (k t) -> p k t", k=16)
          o1 = 16 * GX
          qiT = REG[:, o1:o1 + 8 * G].rearrange("p (k t) -> p k t", k=8)
          o2 = o1 + 8 * G
          uT = REG[:, o2:o2 + 8 * GX].rearrange("p (k t) -> p k t", k=8)
          gT = REG[:, :].rearrange("p (k t) -> p k t", k=NFF)
          xT1 = REG[:, 0:2 * 16 * NT].rearrange("p (b k t) -> p b k t", b=2, k=16)
          wk = REG[:, 4096:4096 + 16 * 320].rearrange("p (k n) -> p k n", k=16)
          qabs = cat[:, :, :].rearrange("p (c h) t -> p c h t", c=2)
          scores = SCR[:, 0:NK]
          scb = [SCR[:, 0:NK], SCR[:, 3712:3712 + NK]]
          nmT = SCR[:, NK:NK + NK // 2].bitcast(BF16).rearrange("p (k t) -> p k t", k=KC)
          rd = SCR[:, NK + NK // 2:NK + NK // 2 + 456]
          xt = SCR[:, 0:D]
          lng = SCR[:, D:2 * D]
          lnb = SCR[:, 2 * D:3 * D]
          tA = SCR[:, 0:2 * GX].rearrange("p (c t) -> p c t", c=2)
          tB = SCR[:, 2 * GX:4 * GX].rearrange("p (c t) -> p c t", c=2)
          zc = SCR[:, 2048:2048 + 4 * G].rearrange("p (b t) -> p b t", b=4)
          gtmp = SCR[:, 4096:4096 + 2 * G].rearrange("p (b t) -> p b t", b=2)

          cnt = st[:, 0:1]
          mid = st[:, 1:2]
          tstep = st[:, 2:3]
          rmin = st[:, 3:4]
          dwid = st[:, 4:5]
          m8 = st[:, 8:16]
          ssq = st[:, 16:17]
          rstd = st[:, 17:18]
          bnst = st[:, 24:48].rearrange("p (a b) -> p a b", a=4)
          mv = st[:, 48:50]
          sd = st[:, 50:51]

          def ld(eng, dst, src, res, slot):
              return P.op(eng, lambda e: e.dma_start(out=dst, in_=src), writes=[res], dma_slot=slot)
          ld("sp", idn[:, :], idn_d[:, :], "idn", "c0")
          ld("pool", ones[:, :], ones_d[:, :], "ones", "c1")
          ld("sp", SCR[0:NT, 1024:1024 + 8 * NT], bd_d[:, :], "scr_bd", "c2")
          ld("sp", SCR[0:NT, 2048:2048 + 8 * NT], bp_d[:, :], "scr_bp", "c3")
          ld("sp", SCR[0:NT, 0:8 * NT], b31_d[:, :], "scr_b31", "c4")
          ld("pool", idnR[:, :], idn_d[:, :], "idnR", "c5")
          ld("sp", trineg[0:NT, :], trineg_d[:, :], "trineg", "c6")
          ld("sp", kflag[0:NT, :], kflag_d[0:1, :].partition_broadcast(NT), "kflag", "c7")
          ld("sp", convw[:, :, :].rearrange("p a b -> p (a b)"), convp[:, :], "convw", "c8")
          ld("sp", pscale[:, :], pscale_d[:, :], "pscale", "c9")
          ld("sp", gkb[0:NT, :], gk_d[0:1, :].partition_broadcast(NT), "gkb", "c10")
          ld("sp", fixc[:, :], fix_d[:, :], "fixc", "c11")
          ld("sp", ln1c[:, :], ln1c_d[:, :], "ln1c", "c14")
          ld("pool", wwi[:, :, :].rearrange("p a b -> p (a b)"), win_w[:, :], "wwi", "c12")
          ld("pool", wk.rearrange("p a b -> p (a b)") if False else REG[:, 4096:4096 + 16 * 320], win_k[:, :], "wk", "c13")
          P.op("pool", lambda e: e.memset(hmask[:, :], 0.0), writes=["hmask"])
          P.op("pool", lambda e: e.memset(hmask[0:64, 0:1], 1.0), writes=["hmask"])
          P.op("pool", lambda e: e.memset(hmask[64:128, 1:2], 1.0), writes=["hmask"])
          for n in range(NIT + 2):
              P.op("pool", lambda e, n=n: e.memset(pw[:, n:n + 1], 2.0 ** (-n)), writes=["pw"])
          P.op("dve", lambda e: e.tensor_copy(out=idnB[:, :], in_=idn[:, :]), reads=["idn"], writes=["idnB"])
          for (c0, Bt, nm, sn) in ((1024, Bd, "Bd", "scr_bd"), (2048, Bp, "Bp", "scr_bp")):
              P.op("dve", lambda e, c0=c0: e.tensor_tensor(out=SCR[0:NT, c0:c0 + 8 * NT], in0=SCR[0:NT, c0:c0 + 8 * NT],
                                                           in1=SCR[0:NT, 0:8 * NT], op=ALU.subtract),
                   reads=["scr_b31"], writes=[sn])
              P.op("dve", lambda e, c0=c0, Bt=Bt: e.tensor_scalar(out=Bt[0:NT].rearrange("p h t -> p (h t)"),
                                                                  in0=SCR[0:NT, c0:c0 + 8 * NT], scalar1=128.0 ** 0.5,
                                                                  scalar2=None, op0=ALU.mult),
                   reads=[sn], writes=[nm])
          P.retire(["scr_b31", "scr_bd", "scr_bp"], ["xt"])
          stop(1, [("Bd", fr(Bd[0:NT]), "Bd"), ("Bp", fr(Bp[0:NT]), "Bp")])
          def transposes_to(ps_list, srcs, rows, cols, reads, tag):
              toks = []
              nb = (len(srcs) + 3) // 4
              for b in range(nb):
                  sub = srcs[4 * b:4 * b + 4]

                  def f(e, sub=sub, b=b):
                      last = None
                      for i, s in enumerate(sub):
                          last = e.transpose(ps_list[b][0:cols, i * rows:(i + 1) * rows], s, idn[0:rows, 0:rows])
                      return last
                  P.op("pe", f, reads=list(reads) + ["idn"], writes=[tag + str(b)])
              return nb

          for kc in range(KC):
              xb = kc % 2
              xtk = xt if xb == 0 else SCR[:, 2048:4096]
              xtn = "xt" if xb == 0 else "xtB"
              P.op("sp", lambda e, kc=kc, xtk=xtk: e.dma_start(out=xtk[0:NT, :], in_=xk[kc * NT:(kc + 1) * NT, :]),
                   writes=[xtn], dma_slot=xtn)
              srcs = [xtk[0:NT, m * 128:(m + 1) * 128] for m in range(16)]
              for b in range(4):
                  def f(e, b=b, srcs=srcs):
                      last = None
                      for i in range(4):
                          last = e.transpose(PS[b][:, i * NT:(i + 1) * NT], srcs[4 * b + i], idn[0:NT, 0:NT])
                      return last
                  P.op("pe", f, reads=[xtn, "idn"], writes=["ps%d" % b])
                  P.op("act" if b % 2 == 0 else "dve",
                       (lambda e, b=b, xb=xb: e.activation(out=xT1[:, xb, 4 * b:4 * b + 4, :],
                                                           in_=PS[b][:, 0:4 * NT].rearrange("p (k t) -> p k t", k=4),
                                                           func=AF.Copy)) if b % 2 == 0 else
                       (lambda e, b=b, xb=xb: e.tensor_copy(out=xT1[:, xb, 4 * b:4 * b + 4, :],
                                                            in_=PS[b][:, 0:4 * NT].rearrange("p (k t) -> p k t", k=4))),
                       reads=["ps%d" % b], writes=["xT1_%d_%d" % (xb, b)])

              pm = 4 + 2 * (kc % 2)
              pmn = "ps%d" % pm
              ptn = "ps%d" % (pm + 1)
              kd = SCR[0:NT, 4096 + 128 * (kc % 2):4096 + 128 * (kc % 2) + 128]
              kdn = "kd%d" % (kc % 2)

              def fmm(e, xb=xb, pm=pm):
                  last = None
                  for k in range(16):
                      last = e.matmul(PS[pm][0:NT, 0:320], xT1[:, xb, k, :], wk[:, k, :], start=(k == 0), stop=(k == 15))
                  return last
              P.op("pe", fmm, reads=["xT1_%d_%d" % (xb, b_) for b_ in range(4)] + ["wk"], writes=[pmn])
              P.op("act", lambda e, pm=pm: e.activation(out=junk[0:NT, 0:256], in_=PS[pm][0:NT, 0:256], func=AF.Square,
                                                        accum_out=ssq[0:NT]),
                   reads=[pmn], writes=["ssq", "junk"])
              P.op("act", lambda e: e.activation(out=sd[0:NT], in_=ssq[0:NT], func=AF.Sqrt, scale=1.0 / 256.0, bias=EPS),
                   reads=["ssq"], writes=["sd"])
              P.op("dve", lambda e: e.reciprocal(out=rstd[0:NT], in_=sd[0:NT]), reads=["sd"], writes=["rstd"])
              P.op("dve", lambda e, kc=kc, pm=pm: e.scalar_tensor_tensor(out=ckv[0:NT, kc, :], in0=PS[pm][0:NT, 0:256],
                                                                         scalar=rstd[0:NT], in1=gkb[0:NT, :],
                                                                         op0=ALU.mult, op1=ALU.mult),
                   reads=[pmn, "rstd", "gkb"], writes=["ckv%d" % kc])
              P.op("act", lambda e, pm=pm, kd=kd: e.activation(out=kd.rearrange("p (a b) -> p a b", a=2),
                                                               in_=PS[pm][0:NT, 256:320].unsqueeze(1).to_broadcast([NT, 2, 64]),
                                                               func=AF.Copy),
                   reads=[pmn], writes=[kdn])

              def small_tr(kc=kc, pm=pm, kd=kd, kdn=kdn, ptn=ptn):
                  def ftr(e):
                      e.transpose(PS[pm + 1][:, 0:NT], fr(ckv[0:NT, kc, 0:128]), idn[0:NT, 0:NT])
                      e.transpose(PS[pm + 1][:, NT:2 * NT], fr(ckv[0:NT, kc, 128:256]), idn[0:NT, 0:NT])
                      return e.transpose(PS[pm + 1][:, 2 * NT:3 * NT], kd, idn[0:NT, 0:NT])
                  P.op("pe", ftr, reads=["ckv%d" % kc, kdn, "idn"], writes=[ptn])
                  P.op("act", lambda e: e.activation(out=ckvT[:, :, kc * NT:(kc + 1) * NT],
                                                     in_=PS[pm + 1][:, 0:2 * NT].rearrange("p (c t) -> p c t", c=2),
                                                     func=AF.Copy),
                       reads=[ptn], writes=["ckvT%d" % kc])
                  P.op("dve", lambda e: e.tensor_copy(out=kT2[:, kc * NT:(kc + 1) * NT], in_=PS[pm + 1][:, 2 * NT:3 * NT]),
                       reads=[ptn], writes=["kT2_%d" % kc])
              if kc > 0:
                  deferred_tr()
              deferred_tr = small_tr
          deferred_tr()
          CKV_ALL = ["ckv%d" % k for k in range(KC)]
          CKVT_ALL = ["ckvT%d" % k for k in range(KC)]
          KT2_ALL = ["kT2_%d" % k for k in range(KC)]

          stop(2, [("ckv", fr(ckv[0:NT]), r) for r in CKV_ALL[-1:]] + [("kT2", fr(kT2[:, :]), KT2_ALL[-1]), ("ckvT", fr(ckvT[:, :, :]), CKVT_ALL[-1])])
          out_toks = []
          wslot = [0]

          def stream_w(src_ap, nelem=2048, three=False):
              ns = 3 if three else 2
              s = wslot[0] % ns
              wslot[0] += 1
              res = "wb%d" % s if s < 2 else "wsm"
              dst = wb[:, s, :] if s < 2 else wsm[:, :]
              P.op("pool", lambda e: e.dma_start(out=dst[:, 0:nelem], in_=src_ap), writes=[res], dma_slot=res)
              return dst, res

          for g in range(NGR):
              ck0 = KOFF + g * G
              P.retire(["gT%d" % i for i in range(NFF)] + ["xT1_%d_%d" % (a_, b_) for a_ in range(2) for b_ in range(4)] + ["wk", "mixT"], ["xTe0", "xTe1", "xTe2", "xTe3", "qiT", "uT"])
              P.retire(["scores0", "scores1", "nmT", "rd", "lng", "lnb", "zc0", "zc1", "zc2", "zc3", "gtmp0", "gtmp1", "tA", "tB", "kd0", "kd1", "xtB"], ["xt"])
              pieces = [(ck0 - HALO, HALO, 0)] + [(ck0 + j * NT, NT, HALO + j * NT) for j in range(TPG)]
              P.retire(["xt"], ["xtB"])
              for pi, (r0, nr, c0) in enumerate(pieces):
                  xta = xt if pi % 2 == 0 else SCR[:, 2048:4096]
                  xtn = "xt" if pi % 2 == 0 else "xtB"
                  P.op("sp", lambda e, r0=r0, nr=nr, xta=xta: e.dma_start(out=xta[0:nr, :], in_=xk[r0:r0 + nr, :]),
                       writes=[xtn], dma_slot=xtn)
                  for b in range(4):
                      def f(e, b=b, nr=nr, xta=xta):
                          last = None
                          for i in range(4):
                              m = 4 * b + i
                              last = e.transpose(PS[b][:, i * nr:(i + 1) * nr], xta[0:nr, m * 128:(m + 1) * 128],
                                                 idn[0:nr, 0:nr])
                          return last
                      P.op("pe", f, reads=[xtn, "idn"], writes=["ps%d" % b])
                      if b % 2 == 0:
                          P.op("act", lambda e, b=b, nr=nr, c0=c0: e.activation(
                              out=xTe[:, 4 * b:4 * b + 4, c0:c0 + nr],
                              in_=PS[b][:, 0:4 * nr].rearrange("p (k t) -> p k t", k=4), func=AF.Copy),
                              reads=["ps%d" % b], writes=["xTe%d" % b])
                      else:
                          P.op("dve", lambda e, b=b, nr=nr, c0=c0: e.tensor_copy(
                              out=xTe[:, 4 * b:4 * b + 4, c0:c0 + nr],
                              in_=PS[b][:, 0:4 * nr].rearrange("p (k t) -> p k t", k=4)),
                              reads=["ps%d" % b], writes=["xTe%d" % b])
              wukS = REG[:, 11328:11328 + 2048]
              P.retire(["gT%d" % i for i in range(NFF)] + ["wk", "diagW", "qz"], ["wukS"])
              P.op("pool", lambda e: e.dma_start(out=wukS, in_=wukT[:, :]), writes=["wukS"], dma_slot="wukS")
              for j in range(TPG):
                  def fw(e, j=j):
                      last = None
                      for k in range(16):
                          last = e.matmul(PS[6][0:NT, 0:16], xTe[:, k, HALO + j * NT:HALO + (j + 1) * NT], wwi[:, k, :],
                                          start=(k == 0), stop=(k == 15))
                      return last
                  P.op("pe", fw, reads=["xTe0", "xTe1", "xTe2", "xTe3"] + ["wwi"], writes=["ps6"])
                  P.op("dve", lambda e, j=j: e.tensor_copy(out=widx[0:NT, j, :], in_=PS[6][0:NT, 0:16]),
                       reads=["ps6"], writes=["widx"])
              P.retire(["cat"], ["qabs"])
              for ci in range(24):
                  s, res = stream_w(win_m[ci, :, :], three=True)
                  pb = 4 + (ci % 2)
                  isu = 8 <= ci < 16
                  n0 = 0 if isu else HALO

                  def fm(e, s=s, pb=pb, n0=n0):
                      last = None
                      for k in range(16):
                          last = e.matmul(PS[pb][:, 0:GX - n0], s[:, k * 128:(k + 1) * 128], xTe[:, k, n0:GX],
                                          start=(k == 0), stop=(k == 15))
                      return last
                  P.op("pe", fm, reads=[res] + ["xTe0", "xTe1", "xTe2", "xTe3"], writes=["ps%d" % pb])
                  if ci < 8:
                      P.op("act", lambda e, ci=ci, pb=pb: e.activation(out=qiT[:, ci, :], in_=PS[pb][:, 0:G], func=AF.Copy),
                           reads=["ps%d" % pb], writes=["qiT"])
                  elif isu:
                      P.op("act", lambda e, ci=ci, pb=pb: e.activation(out=uT[:, ci - 8, :], in_=PS[pb][:, 0:GX], func=AF.Copy),
                           reads=["ps%d" % pb], writes=["uT"])
                  else:
                      h = ci - 16
                      qb = h % 2
                      P.op("act", lambda e, pb=pb, qb=qb: e.activation(out=qtmp[:, qb, :], in_=PS[pb][:, 0:G], func=AF.Copy),
                           reads=["ps%d" % pb], writes=["qtmp%d" % qb])
                      for cc in range(2):
                          P.op("pe", lambda e, h=h, cc=cc, qb=qb: e.matmul(
                              PS[6 + cc][:, 0:G], wukS[:, h * 256 + cc * 128:h * 256 + (cc + 1) * 128], qtmp[:, qb, :],
                              start=True, stop=True),
                              reads=["wukS", "qtmp%d" % qb], writes=["ps%d" % (6 + cc)])
                          P.op("dve", lambda e, h=h, cc=cc: e.tensor_copy(out=qabs[:, cc, h, :], in_=PS[6 + cc][:, 0:G]),
                               reads=["ps%d" % (6 + cc)], writes=["qabs"])
              if g == 0:
                  stop(3, [("qabs", fr(qabs), "qabs"), ("qiT", fr(qiT), "qiT"), ("uT", fr(uT), "uT"), ("widx", widx[0:NT], "widx")])
              P.retire(["xt", "xtB", "lng", "lnb"], ["scores0", "scores1", "nmT", "rd"])
              P.retire(["xTe0", "xTe1", "xTe2", "xTe3"], ["olat"])
              P.retire(["gT%d" % i for i in range(NFF)] + ["wk", "wukS"], ["diagW", "qz"])

              def front_ops(j):
                  J = g * TPG + j
                  scores = scb[j % 2]
                  SC = "scores%d" % (j % 2)
                  nch = 11 + J
                  nk = nch * NT
                  qc0 = j * NT
                  dcol = (nch - 1) * NT
                  nbk = (nk + 455) // 456
                  itc = [0]

                  diagW = REG[0:NT, 11328:11328 + 16 * NT].rearrange("p (h q) -> p h q", h=16)
                  qz = REG[:, 11328 + 16 * NT:11328 + 32 * NT].rearrange("p (c t q) -> p c t q", c=8, t=2)
                  seq = [(bk, h) for bk in range(nbk) for h in range(16)]

                  def dots(i):
                      bk, h = seq[i]
                      hc, hp = h // 2, (h % 2) * 64
                      c0 = bk * 456
                      w = min(456, nk - c0)
                      pb = i % 2
                      P.op("pe", lambda e: e.matmul(
                          PS[5 + pb][0:NT, 0:w], qz[:, hc, h % 2, :], kT2[:, c0:c0 + w], start=True, stop=True),
                          reads=["qz"] + KT2_ALL[c0 // NT:(c0 + w) // NT], writes=["ps%d" % (5 + pb)])
                      P.op("act", lambda e: e.activation(out=rl[0:NT, pb, 0:w], in_=PS[5 + pb][0:NT, 0:w], func=AF.Relu),
                           reads=["ps%d" % (5 + pb)], writes=["rl%d" % pb])

                  def accum(i):
                      bk, h = seq[i]
                      c0 = bk * 456
                      w = min(456, nk - c0)
                      pb = i % 2
                      P.op("pe", lambda e: e.matmul(PS[7][0:NT, 0:w], diagW[:, h, :], rl[0:NT, pb, 0:w],
                                                    start=(h == 0), stop=(h == 15)),
                           reads=["rl%d" % pb, "diagW"], writes=["ps7"])
                      if h == 15:
                          P.op("act", lambda e: e.activation(out=scores[0:NT, c0:c0 + w], in_=PS[7][0:NT, 0:w], func=AF.Copy),
                               reads=["ps7"], writes=[SC])

                  def item(i):
                      if i == 0:
                          P.op("dve", lambda e: e.tensor_tensor(
                              out=diagW, in0=idn[0:NT, 0:NT].unsqueeze(1).to_broadcast([NT, 16, NT]),
                              in1=widx[0:NT, j, :].unsqueeze(2).to_broadcast([NT, 16, NT]), op=ALU.mult),
                              reads=["idn", "widx"], writes=["diagW"])
                          P.op("dve", lambda e: e.tensor_tensor(
                              out=qz, in0=fr(qiT[:, :, qc0:qc0 + NT]).unsqueeze(2).to_broadcast([128, 8, 2, NT]),
                              in1=hmask[:, :].unsqueeze(1).unsqueeze(3).to_broadcast([128, 8, 2, NT]), op=ALU.mult),
                              reads=["qiT", "hmask"], writes=["qz"])
                          dots(0)
                      if i + 1 < len(seq):
                          dots(i + 1)
                      accum(i)
                  items = [(lambda i=i: item(i)) for i in range(len(seq))]

                  Bl, Tl = [], []

                  def build_tail():
                      OP = lambda *a, **k: Bl.append((a, k))
                      OT = lambda *a, **k: Tl.append((a, k))
                      OP("dve", lambda e: e.max(out=m8[0:NT], in_=scores[0:NT, 0:nk]), reads=[SC], writes=["m8"])
                      OP("dve", lambda e: e.tensor_reduce(out=rmin[0:NT], in_=scores[0:NT, 0:nk], op=ALU.min, axis=AX.X),
                           reads=[SC], writes=["rmin"])
                      OP("dve", lambda e: e.tensor_tensor(out=dwid[0:NT], in0=m8[0:NT, 0:1], in1=rmin[0:NT], op=ALU.subtract),
                           reads=["m8", "rmin"], writes=["dwid"])
                      OP("dve", lambda e: e.tensor_scalar(out=dwid[0:NT], in0=dwid[0:NT], scalar1=1.002, scalar2=2e-6,
                                                            op0=ALU.mult, op1=ALU.add), reads=["dwid"], writes=["dwid"])
                      OP("dve", lambda e: e.tensor_scalar(out=wcols[0:NT, :], in0=pw[0:NT, :], scalar1=dwid[0:NT], scalar2=None,
                                                            op0=ALU.mult), reads=["pw", "dwid"], writes=["wcols"])
                      OP("dve", lambda e: e.scalar_tensor_tensor(out=mid[0:NT], in0=dwid[0:NT], scalar=-0.001, in1=rmin[0:NT],
                                                                   op0=ALU.mult, op1=ALU.add),
                           reads=["rmin", "dwid"], writes=["mid"])
                      OP("dve", lambda e: e.tensor_tensor(out=mid[0:NT], in0=mid[0:NT], in1=wcols[0:NT, 1:2], op=ALU.add),
                           reads=["wcols"], writes=["mid"])
                      OP("dve", lambda e: e.scalar_tensor_tensor(out=scores[0:NT, 0:KOFF], in0=kflag[0:NT, :], scalar=NEG,
                                                                   in1=scores[0:NT, 0:KOFF], op0=ALU.mult, op1=ALU.add),
                           reads=["kflag", "rmin", "m8"], writes=[SC])
                      OP("dve", lambda e: e.tensor_tensor(out=scores[0:NT, dcol:dcol + NT], in0=scores[0:NT, dcol:dcol + NT],
                                                            in1=trineg[0:NT, :], op=ALU.add),
                           reads=["trineg"], writes=[SC])
                      for n in range(1, NIT + 1):
                          OP("dve", lambda e: e.tensor_scalar(out=junk[0:NT, 0:nk], in0=scores[0:NT, 0:nk], scalar1=mid[0:NT],
                                                                scalar2=None, op0=ALU.is_ge, op1=ALU.add, accum_out=cnt[0:NT]),
                               reads=[SC, "mid"], writes=["cnt", "junk"])
                          OP("dve", lambda e, n=n: e.tensor_scalar(out=tstep[0:NT], in0=cnt[0:NT], scalar1=255.5,
                                                                     scalar2=wcols[0:NT, n:n + 1], op0=ALU.is_ge, op1=ALU.mult),
                               reads=["cnt", "wcols"], writes=["tstep"])
                          nn = n + 1 if n < NIT else n
                          OP("dve", lambda e, nn=nn: e.scalar_tensor_tensor(out=mid[0:NT], in0=tstep[0:NT], scalar=wcols[0:NT, nn:nn + 1],
                                                                              in1=mid[0:NT], op0=ALU.subtract, op1=ALU.add),
                               reads=["tstep", "wcols"], writes=["mid"])
                      OP("dve", lambda e: e.tensor_scalar(out=scores[0:NT, 0:nk], in0=scores[0:NT, 0:nk], scalar1=mid[0:NT],
                                                            scalar2=None, op0=ALU.is_ge),
                           reads=["mid"], writes=[SC])
                      nb4 = (nch + 3) // 4
                      for b in range(nb4):
                          k0 = 4 * b
                          k1 = min(nch, k0 + 4)

                          def ft(e, k0=k0, k1=k1, b=b):
                              last = None
                              for i, kc in enumerate(range(k0, k1)):
                                  last = e.transpose(PS[5 + (b % 2)][0:NT, i * NT:(i + 1) * NT], scores[0:NT, kc * NT:(kc + 1) * NT],
                                                     idn[0:NT, 0:NT])
                              return last
                          OT("pe", ft, reads=[SC, "idn"], writes=["ps%d" % (5 + b % 2)])
                          OT("act", lambda e, k0=k0, k1=k1, b=b: e.activation(
                              out=nmT[0:NT, k0:k1, :], in_=PS[5 + (b % 2)][0:NT, 0:(k1 - k0) * NT].rearrange("p (k t) -> p k t", k=k1 - k0),
                              func=AF.Identity, scale=MBIG, bias=-MBIG), reads=["ps%d" % (5 + b % 2)], writes=["nmT"])
                  build_tail()
                  return items, Bl, Tl

              def back_ops(j):
                  J = g * TPG + j
                  nch = 11 + J
                  qc0 = j * NT
                  its = []
                  cnt_it = [0]

                  def chunk_l(hh, kc, pb):
                      def fl(e):
                          near = kc >= nch - 2
                          e.matmul(PS[pb][0:NT, 0:456], ckvT[:, 0, kc * NT:(kc + 1) * NT],
                                   qabs[:, 0, 4 * hh:4 * hh + 4, qc0:qc0 + NT], start=True, stop=False)
                          e.matmul(PS[pb][0:NT, 0:456], ckvT[:, 1, kc * NT:(kc + 1) * NT],
                                   qabs[:, 1, 4 * hh:4 * hh + 4, qc0:qc0 + NT], start=False, stop=False)
                          if near:
                              Bt = Bd if kc == nch - 1 else Bp
                              e.matmul(PS[pb][0:NT, 0:456], idnR[0:NT, 0:NT], Bt[0:NT, 4 * hh:4 * hh + 4, :], start=False, stop=False)
                          return e.matmul(PS[pb][0:NT, 0:456], idnB[0:NT, 0:NT],
                                          nmT[0:NT, kc, :].unsqueeze(1).to_broadcast([NT, 4, NT]), start=False, stop=True)
                      P.op("pe", fl, reads=["ckvT%d" % kc, "qabs", "nmT", "Bd", "Bp", "idnR", "idnB"], writes=["ps%d" % pb])
                      P.op("act", lambda e: e.activation(out=em[0:NT, pb, :], in_=PS[pb][0:NT, 0:456], func=AF.Exp,
                                                         scale=128.0 ** -0.5),
                           reads=["ps%d" % pb], writes=["em%d" % pb])

                  def chunk_pv(hh, kc, pb):
                      def fpv(e):
                          st_, sp_ = (kc == 0), (kc == nch - 1)
                          e.matmul(PS[2][:, 0:456], ckv[0:NT, kc, 0:128], em[0:NT, pb, :], start=st_, stop=sp_)
                          e.matmul(PS[3][:, 0:456], ckv[0:NT, kc, 128:256], em[0:NT, pb, :], start=st_, stop=sp_)
                          return e.matmul(PS[4][:, 0:456], ones[0:NT, :], em[0:NT, pb, :], start=st_, stop=sp_)
                      P.op("pe", fpv, reads=["ckv%d" % kc, "em%d" % pb, "ones"], writes=["ps2", "ps3", "ps4"])
                      if kc == nch - 1:
                          P.op("dve", lambda e: e.reciprocal(out=rd[:, :], in_=PS[4][:, 0:456]), reads=["ps4"], writes=["rd"])
                          for cc in range(2):
                              P.op("dve", lambda e, cc=cc: e.tensor_tensor(
                                  out=olat[:, cc, 4 * hh:4 * hh + 4, qc0:qc0 + NT],
                                  in0=PS[2 + cc][:, 0:456].rearrange("p (h t) -> p h t", h=4),
                                  in1=rd[:, :].rearrange("p (h t) -> p h t", h=4), op=ALU.mult),
                                  reads=["ps%d" % (2 + cc), "rd"], writes=["olat"])
                  seq = [(hh, kc, i % 2) for i, (hh, kc) in enumerate((hh, kc) for hh in range(2) for kc in range(nch))]
                  for i, (hh, kc, pb) in enumerate(seq):
                      def it(i=i):
                          if i == 0:
                              chunk_l(*seq[0])
                          if i + 1 < len(seq):
                              chunk_l(*seq[i + 1])
                          chunk_pv(*seq[i])
                      its.append(it)
                  return its

              fr_ = [front_ops(j) for j in range(TPG)]
              bk_ = [None] * TPG

              def emit_all(ops):
                  for a, k in ops:
                      P.op(*a, **k)
              for t in range(TPG + 2):
                  Il = fr_[t][0] if t < TPG else []
                  Bl = fr_[t - 1][1] if 0 <= t - 1 < TPG else []
                  Tl = fr_[t - 1][2] if 0 <= t - 1 < TPG else []
                  if 0 <= t - 2 < TPG:
                      bk_[t - 2] = back_ops(t - 2)
                      Kl = bk_[t - 2]
                  else:
                      Kl = []
                  lead = Il if Il else (Kl if Kl else [None])
                  n = len(lead)
                  dk = db = 0
                  for i in range(n):
                      if Il:
                          Il[i]()
                      wk_ = (len(Kl) * (i + 1)) // n
                      while dk < wk_:
                          Kl[dk]()
                          dk += 1
                      wb_ = (len(Bl) * (i + 1)) // n
                      while db < wb_:
                          a, k = Bl[db]
                          P.op(*a, **k)
                          db += 1
                  emit_all(Tl)
                  if g == 0 and t == 1:
                      stop(4, [("mask", scb[0][0:NT, 0:11 * NT], "scores0"), ("thr", mid[0:NT], "mid")])
              if g == 0:
                  stop(5, [("olat", fr(olat), "olat")])
              P.op("pool", lambda e: e.dma_start(out=wsm[:, :], in_=wuv[:, :]), writes=["wsm"], dma_slot="wsm")
              P.retire(["qabs"], ["cat"])
              for h in range(8):
                  pb = 6 + (h % 2)

                  def fa(e, h=h, pb=pb):
                      e.matmul(PS[pb][:, 0:G], wsm[:, (0 * 8 + h) * 128:(0 * 8 + h + 1) * 128], olat[:, 0, h, :], start=True, stop=False)
                      return e.matmul(PS[pb][:, 0:G], wsm[:, (1 * 8 + h) * 128:(1 * 8 + h + 1) * 128], olat[:, 1, h, :],
                                      start=False, stop=True)
                  P.op("pe", fa, reads=["wsm", "olat"], writes=["ps%d" % pb])
                  P.op("act", lambda e, h=h, pb=pb: e.activation(out=cat[:, h, :], in_=PS[pb][:, 0:G], func=AF.Copy),
                       reads=["ps%d" % pb], writes=["cat"])
              P.op("pool", lambda e: e.dma_start(out=wsm[:, :], in_=wpool[:, :]), writes=["wsm"], dma_slot="wsm")
              P.retire(["scores0", "scores1", "nmT", "rd"], ["tA", "tB"])
              for gi in range(4):
                  cur = fr(uT[:, 2 * gi:2 * gi + 2, :])
                  bufs = [tA, tB]
                  for lev in range(gi + 1):
                      sh = 1 << lev
                      nxt = bufs[lev % 2]
                      vs = sh - 1
                      P.op("dve", lambda e, cur=cur, nxt=nxt, sh=sh, vs=vs: e.tensor_tensor(
                          out=nxt[:, :, vs + sh:GX], in0=cur[:, :, vs + sh:GX], in1=cur[:, :, vs:GX - sh], op=ALU.add),
                          reads=["uT", "tA", "tB"], writes=["tA" if lev % 2 == 0 else "tB"])
                      cur = nxt
                  wn = "tA" if gi % 2 == 0 else "tB"
                  if gi == 3 and g == 0:
                      P.op("dve", lambda e, cur=cur: e.tensor_scalar(out=cur[:, :, HALO:HALO + 1], in0=cur[:, :, HALO:HALO + 1],
                                                                     scalar1=fixc[:, 0:1], scalar2=None, op0=ALU.mult),
                           reads=["fixc"], writes=[wn])
                  P.op("dve", lambda e, cur=cur, gi=gi: e.scalar_tensor_tensor(
                      out=dT[:, :, :], in0=cur[:, :, HALO:GX], scalar=1.0 / (2 << gi), in1=fr(uT[:, 2 * gi:2 * gi + 2, HALO:GX]),
                      op0=ALU.mult, op1=ALU.subtract), reads=[wn, "uT"], writes=["dT"])
                  for oc in range(2):
                      pb = 6 + oc

                      def fp(e, gi=gi, oc=oc, pb=pb):
                          b0 = (gi * 2 + 0) * 256 + oc * 128
                          b1 = (gi * 2 + 1) * 256 + oc * 128
                          e.matmul(PS[pb][:, 0:G], wsm[:, b0:b0 + 128], dT[:, 0, :], start=True, stop=False)
                          return e.matmul(PS[pb][:, 0:G], wsm[:, b1:b1 + 128], dT[:, 1, :], start=False, stop=True)
                      P.op("pe", fp, reads=["wsm", "dT"], writes=["ps%d" % pb])
                      P.op("act", lambda e, gi=gi, oc=oc, pb=pb: e.activation(
                          out=cat[:, 8 + gi * 2 + oc, :], in_=PS[pb][:, 0:G], func=AF.Identity,
                          scale=pscale[:, gi * 2 + oc:gi * 2 + oc + 1]),
                          reads=["ps%d" % pb, "pscale"], writes=["cat"])
              if g == 0:
                  stop(6, [("cat", fr(cat[:, :, :]), "cat")])
              P.retire(["olat"] + ["xTe0", "xTe1", "xTe2", "xTe3"], ["mixT"])
              for m in range(16):
                  s, res = stream_w(wo[m, :, :], three=True)
                  pb = 4 + (m % 2)

                  def fo(e, s=s, pb=pb):
                      last = None
                      for k in range(16):
                          last = e.matmul(PS[pb][:, 0:G], s[:, k * 128:(k + 1) * 128], cat[:, k, :],
                                          start=(k == 0), stop=(k == 15))
                      return last
                  P.op("pe", fo, reads=[res, "cat"], writes=["ps%d" % pb])
                  P.op("act", lambda e, m=m, pb=pb: e.activation(out=mixT[:, m, :], in_=PS[pb][:, 0:G], func=AF.Copy),
                       reads=["ps%d" % pb], writes=["mixT"])
              if g == 0:
                  stop(7, [("mixT", fr(mixT), "mixT")])
              P.retire(["tA", "tB", "scores0", "scores1", "nmT", "rd"], ["xt", "lng", "lnb"])
              P.retire(["cat"], ["h1T"])

              def layer_norm_tile(src_views, resid_fn, W, wn, par, affine):
                  S = st if par == 0 else st2
                  bnst_, mv_, sd_, rstd_ = (S[:, 24:48].rearrange("p (a b) -> p a b", a=4), S[:, 48:50], S[:, 50:51], S[:, 17:18])
                  sn = "lnst%d" % par
                  for b in range(4):
                      def f(e, b=b):
                          last = None
                          for i in range(4):
                              last = e.transpose(PS[b][0:NT, i * 128:(i + 1) * 128], src_views[4 * b + i], idn[:, :])
                          return last
                      P.op("pe", f, reads=["mixT", "h1T", "idn"], writes=["ps%d" % b])
                      if resid_fn is not None:
                          resid_fn(b)
                  for b in range(4):
                      if resid_fn is not None:
                          P.op("dve", lambda e, b=b: e.bn_stats(out=bnst_[0:NT, b, :], in_=W[0:NT, b * 512:(b + 1) * 512]),
                               reads=[wn], writes=[sn])
                      else:
                          P.op("dve", lambda e, b=b: e.bn_stats(out=bnst_[0:NT, b, :], in_=PS[b][0:NT, :]),
                               reads=["ps%d" % b], writes=[sn])
                  P.op("dve", lambda e: e.bn_aggr(out=mv_[0:NT], in_=bnst_[0:NT].rearrange("p a b -> p (a b)")),
                       reads=[sn], writes=[sn])
                  P.op("act", lambda e: e.activation(out=sd_[0:NT], in_=mv_[0:NT, 1:2], func=AF.Sqrt, bias=EPS),
                       reads=[sn], writes=[sn])
                  P.op("dve", lambda e: e.reciprocal(out=rstd_[0:NT], in_=sd_[0:NT]), reads=[sn], writes=[sn])
                  nmr_ = S[:, 18:19]
                  if resid_fn is None:
                      P.op("dve", lambda e: e.scalar_tensor_tensor(out=nmr_[0:NT], in0=mv_[0:NT, 0:1], scalar=-1.0, in1=rstd_[0:NT],
                                                                   op0=ALU.mult, op1=ALU.mult), reads=[sn], writes=[sn])
                  if resid_fn is not None:
                      P.op("dve", lambda e: e.tensor_scalar(out=W[0:NT, :], in0=W[0:NT, :], scalar1=mv_[0:NT, 0:1], scalar2=rstd_[0:NT],
                                                            op0=ALU.subtract, op1=ALU.mult),
                           reads=[sn], writes=[wn])
                  else:
                      for b in range(4):
                          P.op("dve" if b % 2 == 0 else "act",
                               (lambda e, b=b: e.tensor_scalar(out=W[0:NT, b * 512:(b + 1) * 512], in0=PS[b][0:NT, :],
                                                               scalar1=mv_[0:NT, 0:1], scalar2=rstd_[0:NT],
                                                               op0=ALU.subtract, op1=ALU.mult)) if b % 2 == 0 else
                               (lambda e, b=b: e.activation(out=W[0:NT, b * 512:(b + 1) * 512], in_=PS[b][0:NT, :],
                                                            func=AF.Identity, scale=rstd_[0:NT], bias=nmr_[0:NT])),
                               reads=[sn, "ps%d" % b], writes=[wn + "_%d" % b])
                  if affine:
                      for b in range(4):
                          P.op("dve", lambda e, b=b: e.tensor_tensor(out=W[0:NT, b * 512:(b + 1) * 512], in0=W[0:NT, b * 512:(b + 1) * 512],
                                                                     in1=lng[0:NT, b * 512:(b + 1) * 512], op=ALU.mult),
                               reads=["lng"], writes=[wn + "_%d" % b])
                          P.op("pool" if b % 2 == 0 else "dve", lambda e, b=b: e.tensor_tensor(
                              out=W[0:NT, b * 512:(b + 1) * 512], in0=W[0:NT, b * 512:(b + 1) * 512],
                              in1=lnb[0:NT, b * 512:(b + 1) * 512], op=ALU.add),
                              reads=["lnb"], writes=[wn + "_%d" % b])

              LW1 = [(xt, "xt"), (lng, "lng"), (lnb, "lnb")]
              for j in range(TPG):
                  r0 = ck0 + j * NT
                  W, wn = LW1[j]
                  P.op("sp", lambda e, r0=r0, W=W: e.dma_start(out=W[0:NT, :], in_=xk[r0:r0 + NT, :]), writes=[wn], dma_slot=wn)
              for j in range(TPG):
                  W, wn = LW1[j]

                  def resid1(b, W=W, wn=wn):
                      P.op("dve", lambda e, b=b: e.scalar_tensor_tensor(
                          out=W[0:NT, b * 512:(b + 1) * 512], in0=W[0:NT, b * 512:(b + 1) * 512], scalar=ALPHA,
                          in1=PS[b][0:NT, :], op0=ALU.mult, op1=ALU.add), reads=["ps%d" % b], writes=[wn])
                  layer_norm_tile([fr(mixT[:, m, j * NT:(j + 1) * NT]) for m in range(16)], resid1, W, wn, j % 2, False)
                  for b in range(4):
                      def f(e, b=b, W=W):
                          last = None
                          for i in range(4):
                              m = 4 * b + i
                              last = e.transpose(PS[4 + (b % 2)][:, i * NT:(i + 1) * NT], W[0:NT, m * 128:(m + 1) * 128],
                                                 idn[0:NT, 0:NT])
                          return last
                      P.op("pe", f, reads=[wn, "idn"], writes=["ps%d" % (4 + b % 2)])
                      for i in range(4):
                          m = 4 * b + i
                          P.op("act", lambda e, b=b, i=i, m=m, j=j: e.activation(
                              out=cat[:, m, j * NT:(j + 1) * NT], in_=PS[4 + (b % 2)][:, i * NT:(i + 1) * NT],
                              func=AF.Identity, scale=ln1c[:, m:m + 1], bias=ln1c[:, 16 + m:17 + m]),
                              reads=["ps%d" % (4 + b % 2), "ln1c"], writes=["h1T"])
              if g == 0:
                  stop(8, [("h1T", fr(cat[:, :, :]), "h1T")])
              P.retire(["mixT", "qiT", "uT", "olat"] + ["xTe0", "xTe1", "xTe2", "xTe3"], ["gT%d" % i for i in range(NFF)])
              P.retire(["xt", "lng", "lnb", "tA", "tB", "scores0", "scores1", "nmT", "rd"], ["zc0", "zc1", "zc2", "zc3", "gtmp0", "gtmp1"])
              for i in range(NFF if not os.environ.get('SKIPUP') else 0):
                  for half in range(2):
                      cidx = i + NFF * half
                      s, res = stream_w(wup[cidx, :, :], three=True)
                      pb = 4 + 2 * (i % 2) + half
                      zb = 2 * (i % 2) + half

                      def fu(e, s=s, pb=pb):
                          last = None
                          for k in range(16):
                              last = e.matmul(PS[pb][:, 0:G], s[:, k * 128:(k + 1) * 128], cat[:, k, :],
                                              start=(k == 0), stop=(k == 15))
                          return last
                      P.op("pe", fu, reads=[res, "h1T"], writes=["ps%d" % pb])
                      zr = "zc%d" % zb
                      P.op("act", lambda e, cidx=cidx, pb=pb, zb=zb: e.activation(
                          out=zc[:, zb, :], in_=PS[pb][:, 0:G], func=AF.Identity, scale=convw[:, cidx, 2:3], bias=convw[:, cidx, 3:4]),
                          reads=["ps%d" % pb, "convw"], writes=[zr])
                      P.op("dve", lambda e, cidx=cidx, pb=pb, zb=zb: e.scalar_tensor_tensor(
                          out=zc[:, zb, 1:G], in0=PS[pb][:, 0:G - 1], scalar=convw[:, cidx, 1:2], in1=zc[:, zb, 1:G],
                          op0=ALU.mult, op1=ALU.add), reads=["ps%d" % pb, "convw"], writes=[zr])
                      P.op("dve", lambda e, cidx=cidx, pb=pb, zb=zb: e.scalar_tensor_tensor(
                          out=zc[:, zb, 2:G], in0=PS[pb][:, 0:G - 2], scalar=convw[:, cidx, 0:1], in1=zc[:, zb, 2:G],
                          op0=ALU.mult, op1=ALU.add), reads=["ps%d" % pb, "convw"], writes=[zr])
                      if g > 0:
                          P.op("dve", lambda e, cidx=cidx, zb=zb: e.scalar_tensor_tensor(
                              out=zc[:, zb, 0:2], in0=halo[:, cidx, 0:2], scalar=convw[:, cidx, 0:1], in1=zc[:, zb, 0:2],
                              op0=ALU.mult, op1=ALU.add), reads=["halo%d" % cidx, "convw"], writes=[zr])
                          P.op("dve", lambda e, cidx=cidx, zb=zb: e.scalar_tensor_tensor(
                              out=zc[:, zb, 0:1], in0=halo[:, cidx, 1:2], scalar=convw[:, cidx, 1:2], in1=zc[:, zb, 0:1],
                              op0=ALU.mult, op1=ALU.add), reads=["halo%d" % cidx, "convw"], writes=[zr])
                      P.op("act", lambda e, cidx=cidx, pb=pb: e.activation(out=halo[:, cidx, :], in_=PS[pb][:, G - 2:G], func=AF.Copy),
                           reads=["ps%d" % pb], writes=["halo%d" % cidx])
                  za, zg = 2 * (i % 2), 2 * (i % 2) + 1
                  gb = i % 2
                  P.op("act", lambda e, za=za, gb=gb: e.activation(out=gtmp[:, gb, :], in_=zc[:, za, :], func=AF.Gelu_apprx_tanh),
                       reads=["zc%d" % za], writes=["gtmp%d" % gb])
                  P.op("dve", lambda e, i=i, zg=zg, gb=gb: e.tensor_tensor(out=gT[:, i, :], in0=gtmp[:, gb, :], in1=zc[:, zg, :], op=ALU.mult),
                       reads=["gtmp%d" % gb, "zc%d" % zg], writes=["gT%d" % i])
              if g == 0:
                  stop(9, [("gT", fr(gT[:, :, :]), "gT%d" % (NFF - 1))])
              for m in range(16):
                  pb = 4 + (m % 2)
                  for qd in range(4):
                      s, res = stream_w(wdn[m, qd, :, :], nelem=11 * 128, three=True)

                      def fd(e, s=s, pb=pb, qd=qd):
                          last = None
                          for k in range(11):
                              kk = qd * 11 + k
                              last = e.matmul(PS[pb][:, 0:G], s[:, k * 128:(k + 1) * 128], gT[:, kk, :],
                                              start=(kk == 0), stop=(kk == NFF - 1))
                          return last
                      P.op("pe", fd, reads=[res] + ["gT%d" % (qd * 11 + k) for k in range(11)], writes=["ps%d" % pb])
                  P.op("dve", lambda e, m=m, pb=pb: e.scalar_tensor_tensor(
                      out=cat[:, m, :], in0=fr(cat[:, m, :]), scalar=ALPHA, in1=PS[pb][:, 0:G], op0=ALU.mult, op1=ALU.add),
                      reads=["ps%d" % pb], writes=["h1T"])
              if g == 0:
                  stop(10, [("h2T", fr(cat[:, :, :]), "h1T")])
              P.retire(["zc0", "zc1", "zc2", "zc3", "gtmp0", "gtmp1"], ["xt", "lng", "lnb"])
              P.op("sp", lambda e: e.dma_start(out=lng[0:NT, :], in_=ln_d[2:3, :].partition_broadcast(NT)), writes=["lng"], dma_slot="lng")
              P.op("sp", lambda e: e.dma_start(out=lnb[0:NT, :], in_=ln_d[3:4, :].partition_broadcast(NT)), writes=["lnb"], dma_slot="lnb")
              LW2 = [(xt, "xt"), (xt, "xt")]
              for j in range(TPG):
                  J = g * TPG + j
                  W, wn = LW2[J % 2]

                  XB = [wn + "_%d" % b for b in range(4)]
                  P.retire([wn], XB)
                  layer_norm_tile([fr(cat[:, m, j * NT:(j + 1) * NT]) for m in range(16)], None, W, wn, J % 2, True)
                  if J == 0:
                      t = P.op("sp", lambda e, W=W: e.dma_start(out=out[0:NT - 2, :], in_=W[2:NT, :]), reads=XB, dma_slot="ost%d" % (J % 2))
                  else:
                      o0 = J * NT - 2
                      t = P.op("sp", lambda e, o0=o0, W=W: e.dma_start(out=out[o0:o0 + NT, :], in_=W[0:NT, :]), reads=XB,
                               dma_slot="ost%d" % (J % 2))
                  P.retire(XB, [wn])
                  out_toks.append(t)
              P.retire(["h1T"], ["cat"])
          P.final_wait("sp", out_toks[-2:])
          P.emit()
      except _Done:
        pass
    nc._dbg_dumps = dumps
    return nc


def t5_bucket_np(dist):
    max_exact = 16
    d_f = np.maximum(dist, 1).astype(np.float32)
    large = max_exact + (np.log(d_f / max_exact) / math.log(128 / max_exact) * (32 - max_exact)).astype(np.int32)
    large = np.minimum(large, 31)
    return np.where(dist < max_exact, dist, large)


_NC_CACHE = {}


def prep_shared(meta, rel_bias, w_in, kv_norm_g, w_uk, w_uv, w_pool, pool_scale, w_o, ln1_g, ln1_b,
                w_up, conv_w, conv_b, w_down, ln2_g, ln2_b):
    f = np.float32
    W = np.asarray(w_in[0], f)

    def blk(Wc):
        n = Wc.shape[1]
        return np.ascontiguousarray(Wc.reshape(16, 128, n).transpose(1, 0, 2)).reshape(128, 16 * n)
    cols = [C_QI + 128 * i for i in range(8)] + [C_U + 128 * i for i in range(8)] + [C_Q + 128 * i for i in range(8)]
    sh = {}
    sh["win_m"] = np.stack([blk(W[:, c:c + 128]) for c in cols])
    sh["win_k"] = blk(np.concatenate([W[:, C_KV:C_KV + 256], W[:, C_KI:C_KI + 64]], axis=1))
    sh["win_w"] = blk(W[:, C_WI:C_WI + 16])
    sh["wukT"] = np.ascontiguousarray(np.asarray(w_uk[0], f).transpose(2, 1, 0)).reshape(128, 8 * 256)
    sh["wuv"] = np.ascontiguousarray(np.asarray(w_uv[0], f).reshape(2, 128, 8, 128).transpose(1, 0, 2, 3)).reshape(128, 2048)
    sh["wpool"] = np.ascontiguousarray(np.asarray(w_pool[0], f).reshape(4, 2, 128, 256).transpose(2, 0, 1, 3)).reshape(128, 2048)
    Wo = np.asarray(w_o[0], f)
    sh["wo"] = np.stack([blk(Wo[:, 128 * m:128 * m + 128]) for m in range(16)])
    Wu = np.asarray(w_up[0], f)
    sh["wup"] = np.stack([blk(Wu[:, 128 * m:128 * m + 128]) for m in range(88)])
    Wd = np.asarray(w_down[0], f)
    wd = Wd.reshape(4, 11, 128, 16, 128).transpose(3, 0, 2, 1, 4)
    sh["wdn"] = np.ascontiguousarray(wd).reshape(16, 4, 128, 11 * 128)
    cw = np.asarray(conv_w[0], f)
    cb = np.asarray(conv_b[0], f)
    cp = np.concatenate([cw, cb[None]], axis=0)
    sh["convp"] = np.ascontiguousarray(cp.reshape(4, 88, 128).transpose(2, 1, 0)).reshape(128, 88 * 4)
    sh["pscale"] = np.ascontiguousarray(np.asarray(pool_scale[0], f).reshape(8, 128).T)
    sh["gk"] = np.asarray(kv_norm_g[0], f).reshape(1, 256)
    sh["lnp"] = np.stack([np.asarray(a[0], f) for a in (ln1_g, ln1_b, ln2_g, ln2_b)])
    sh["ln1c"] = np.ascontiguousarray(np.concatenate([np.asarray(ln1_g[0], f).reshape(16, 128).T, np.asarray(ln1_b[0], f).reshape(16, 128).T], axis=1))
    sh["idn"] = np.eye(128, dtype=f)
    sh["ones"] = np.ones((128, 128), f)
    rb = np.asarray(rel_bias, f)
    p = np.arange(NT)[:, None]
    i = np.arange(NT)[None, :]
    dd = t5_bucket_np(np.maximum(i - p, 0))
    dp = t5_bucket_np(i - p + NT)
    sh["bd"] = np.ascontiguousarray(rb[dd].transpose(0, 2, 1)).reshape(NT, 8 * NT)
    sh["bp"] = np.ascontiguousarray(rb[dp].transpose(0, 2, 1)).reshape(NT, 8 * NT)
    sh["b31"] = np.ascontiguousarray(np.broadcast_to(rb[31][None, :, None], (NT, 8, NT))).reshape(NT, 8 * NT)
    sh["tri01"] = (i >= p).astype(f)
    sh["trineg"] = np.where(p >= i, 0.0, NEG).astype(f)
    return sh


def kernel(x, meta, rel_bias, w_in, kv_norm_g, w_uk, w_uv, w_pool, pool_scale, w_o, ln1_g, ln1_b,
           w_up, conv_w, conv_b, w_down, ln2_g, ln2_b):
    x = np.asarray(x, np.float32)
    meta = np.asarray(meta, np.float32)
    sh = prep_shared(meta, rel_bias, w_in, kv_norm_g, w_uk, w_uv, w_pool, pool_scale, w_o, ln1_g, ln1_b,
                     w_up, conv_w, conv_b, w_down, ln2_g, ln2_b)
    if "nc" not in _NC_CACHE:
        _NC_CACHE["nc"] = build_nc()
    nc = _NC_CACHE["nc"]
    in_maps = []
    for c in range(8):
        b, half = c // 2, c % 2
        base = 14 + 1024 * half
        hfull = np.concatenate([meta, x[b]], axis=0)
        xk = np.zeros((NK, D), np.float32)
        tok = base - KOFF + np.arange(NK)
        valid = tok >= 0
        xk[valid] = hfull[tok[valid]]
        m = dict(sh)
        m["xk"] = xk
        m["kflag"] = (~valid[:KOFF]).astype(np.uint8).reshape(1, KOFF)
        m["fixc"] = np.full((128, 1), 16.0 / 15.0 if half == 0 else 1.0, np.float32)
        in_maps.append(m)
    res = run_bass_kernel_spmd(nc, in_maps, core_ids=list(range(8)))
    outp = np.zeros((4, 2048, D), np.float32)
    for c in range(8):
        b, half = c // 2, c % 2
        outp[b, 1024 * half:1024 * (half + 1)] = res.results[c]["out"]
    return outp
```

```python
import math
import numpy as np
import concourse.bass as bass
import concourse.mybir as mybir
from concourse.bass_utils import run_bass_kernel_spmd
from contextlib import ExitStack

F32 = mybir.dt.float32
F32R = mybir.dt.float32r
U8 = mybir.dt.uint8
AF = mybir.ActivationFunctionType
ALU = mybir.AluOpType
AX = mybir.AxisListType

D = 2048
NT = 114
TPG = 3
G = NT * TPG
NGR = 3
NQT = 9
KC = 19
NK = KC * NT
KOFF = 10 * NT
HALO = 16
GX = G + HALO
DFF = 5632
NFF = 44
ALPHA = 2.0 ** 0.25
EPS = 1e-5
NIT = 18
MBIG = 30000.0
BF16 = mybir.dt.bfloat16
NEG = -1.0e30
C_Q, C_KV, C_QI, C_KI, C_WI, C_U = 0, 1024, 1280, 2304, 2368, 2384


class Prog:
    ENG = ("pe", "act", "dve", "pool", "sp")

    def __init__(self, nc, es):
        self.nc = nc
        self.es = es
        self.streams = {e: [] for e in self.ENG}
        self.count = {e: 0 for e in self.ENG}
        self.sems = {}
        for e in self.ENG:
            self.sems[("eng", e)] = es.enter_context(nc.semaphore("sem_" + e))
        self.dma_count = {}
        self.last_w = {}
        self.readers = {}
        self.waited = {e: {} for e in self.ENG}

    def _dma_sem(self, slot):
        k = ("dma", slot)
        if k not in self.sems:
            self.sems[k] = self.es.enter_context(self.nc.semaphore("dsem%d" % len(self.sems)))
            self.dma_count[slot] = 0
        return k

    def retire(self, old, new):
        toks = []
        for n in old:
            t = self.last_w.get(n)
            if t is not None:
                toks.append(t)
            toks.extend(self.readers.get(n, ()))
        for n in new:
            self.readers[n] = list(toks) + list(self.readers.get(n, ()))

    def op(self, eng, fn, reads=(), writes=(), dma_slot=None, n_dma=1):
        writes = list(writes) + [r for r in reads if r.startswith("ps")]
        reads = [r for r in reads if not r.startswith("ps")]
        deps = {}

        def add(tok):
            if tok is None:
                return
            k, v = tok
            if deps.get(k, 0) < v:
                deps[k] = v
        for r in reads:
            add(self.last_w.get(r))
        for w in writes:
            add(self.last_w.get(w))
            for t in self.readers.get(w, ()):
                add(t)
        waits = []
        for k, v in deps.items():
            if k == ("eng", "pe") and eng == "pe":
                continue
            if self.waited[eng].get(k, 0) >= v:
                continue
            self.waited[eng][k] = v
            waits.append((k, v))
        if dma_slot is not None:
            k = self._dma_sem(dma_slot)
            self.dma_count[dma_slot] += 16 * n_dma
            tok = (k, self.dma_count[dma_slot])
            inc = 16
        else:
            self.count[eng] += 1
            tok = (("eng", eng), self.count[eng])
            inc = 1
        self.streams[eng].append((waits, fn, tok[0], inc))
        for r in reads:
            self.readers.setdefault(r, []).append(tok)
        for w in writes:
            self.last_w[w] = tok
            self.readers[w] = []
        return tok

    def final_wait(self, eng, toks):
        self.streams[eng].append(([(k, v) for (k, v) in toks], None, None, 0))

    def emit(self):
        nc = self.nc
        with nc.Block() as block:
            def run(ename):
                def body(e):
                    for waits, fn, semk, inc in self.streams[ename]:
                        for k, v in waits:
                            e.wait_ge(self.sems[k], v)
                        if fn is None:
                            continue
                        res = fn(e)
                        if isinstance(res, (list, tuple)):
                            for r in res:
                                r.then_inc(self.sems[semk], inc)
                        else:
                            res.then_inc(self.sems[semk], inc)
                return body
            block.tensor(run("pe"))
            block.scalar(run("act"))
            block.vector(run("dve"))
            block.gpsimd(run("pool"))
            block.sync(run("sp"))


class _Done(Exception):
    pass


def build_nc(stage=None):
    nc = bass.Bass("TRN2", target_bir_lowering=False)
    dumps = []

    def din(name, shape, dt=F32):
        return nc.dram_tensor(name, list(shape), dt, kind="ExternalInput").ap()
    xk = din("xk", [NK, D])
    win_m = din("win_m", [24, 128, 16 * 128])
    win_k = din("win_k", [128, 16 * 320])
    win_w = din("win_w", [128, 16 * 16])
    wukT = din("wukT", [128, 8 * 256])
    wuv = din("wuv", [128, 2 * 8 * 128])
    wpool = din("wpool", [128, 4 * 2 * 256])
    wo = din("wo", [16, 128, 16 * 128])
    wup = din("wup", [88, 128, 16 * 128])
    wdn = din("wdn", [16, 128, 44 * 128])
    convp = din("convp", [128, 88 * 4])
    pscale_d = din("pscale", [128, 8])
    gk_d = din("gk", [1, 256])
    ln_d = din("lnp", [4, D])
    idn_d = din("idn", [128, 128])
    ones_d = din("ones", [128, 128])
    bd_d = din("bd", [NT, 8 * NT])
    bp_d = din("bp", [NT, 8 * NT])
    b31_d = din("b31", [NT, 8 * NT])
    tri01_d = din("tri01", [NT, NT])
    trineg_d = din("trineg", [NT, NT])
    kflag_d = din("kflag", [1, KOFF], U8)
    fix_d = din("fixc", [128, 1])
    ln1c_d = din("ln1c", [128, 32])
    out = nc.dram_tensor("out", [1024, D], F32, kind="ExternalOutput").ap()

    with ExitStack() as es:
      try:
          P = Prog(nc, es)

          def sb(name, shape, dt=F32):
              return es.enter_context(nc.sbuf_tensor("s_" + name, list(shape), dt))

          def psum(name):
              return es.enter_context(nc.psum_tensor("p_" + name, [128, 512], F32))

          def fr(ap):
              return ap.bitcast(F32)

          def stop(sid, items):
              if stage != sid:
                  return
              toks = []
              for name, ap, res in items:
                  shp = list(ap.shape)
                  d = nc.dram_tensor("dbg_" + name, shp, F32, kind="ExternalOutput").ap()
                  dumps.append("dbg_" + name)
                  toks.append(P.op("sp", lambda e, d=d, ap=ap: e.dma_start(out=d, in_=ap), reads=[res], dma_slot="dbg_" + name))
              P.final_wait("sp", toks)
              P.emit()
              raise _Done()

          ckvT = sb("ckvT", [128, 2, NK], F32R)
          ckv = sb("ckv", [128, KC, 256], F32R)
          kT2 = sb("kT2", [128, NK], F32R)
          wsm = sb("wsm", [128, 2048], F32R)
          wwi = sb("wwi", [128, 16, 16], F32R)
          ones = sb("ones", [128, 128], F32R)
          cat = sb("cat", [128, 16, G], F32R)
          wb = sb("wb", [128, 2, 2048], F32R)
          REG = sb("REG", [128, NFF * G], F32R)
          dT = sb("dT", [128, 2, G], F32R)
          qtmp = sb("qtmp", [128, 2, G], F32R)
          em = sb("em", [128, 2, 456], F32R)
          rl = sb("rl", [128, 2, 456], F32R)
          idnR = sb("idnR", [128, 128], F32R)
          idnB = sb("idnB", [128, 128], BF16)
          idn = sb("idn", [128, 128])
          Bd = sb("Bd", [128, 8, NT], F32R)
          Bp = sb("Bp", [128, 8, NT], F32R)
          kflag = sb("kflag", [128, KOFF], U8)
          trineg = sb("trineg", [128, NT])
          pw = sb("pw", [128, NIT + 2])
          convw = sb("convw", [128, 88, 4])
          pscale = sb("pscale", [128, 8])
          gkb = sb("gkb", [128, 256])
          halo = sb("halo", [128, 88, 2])
          fixc = sb("fixc", [128, 1])
          hmask = sb("hmask", [128, 2])
          junk = sb("junk", [128, NK], U8)
          widx = sb("widx", [128, TPG, 16])
          st = sb("st", [128, 64])
          st2 = sb("st2", [128, 64])
          ln1c = sb("ln1c", [128, 32])
          wcols = sb("wcols", [128, NIT + 2])
          SCR = sb("SCR", [128, 6144])
          PS = [psum("ps%d" % i) for i in range(8)]

          xTe = REG[:, 0:16 * GX].rearrange("p (k t) -> p k t", k=16)
          olat = REG[:, 0:16 * G].rearrange("p (c h t) -> p c h t", c=2, h=8)
          mixT = REG[:, 0:16 * G].rearrange("p (k t) -> p k t", k=16)
          o1 = 16 * GX
          qiT = REG[:, o1:o1 + 8 * G].rearrange("p (k t) -> p k t", k=8)
          o2 = o1 + 8 * G
          uT = REG[:, o2:o2 + 8 * GX].rearrange("p (k t) -> p k t", k=8)
          gT = REG[:, :].rearrange("p (k t) -> p k t", k=NFF)
          xT1 = REG[:, 0:2 * 16 * NT].rearrange("p (b k t) -> p b k t", b=2, k=16)
          wk = REG[:, 4096:4096 + 16 * 320].rearrange("p (k n) -> p k n", k=16)
          qabs = cat[:, :, :].rearrange("p (c h) t -> p c h t", c=2)
          scores = SCR[:, 0:NK]
          scb = [SCR[:, 0:NK], SCR[:, 3712:3712 + NK]]
          nmT = SCR[:, NK:NK + NK // 2].bitcast(BF16).rearrange("p (k t) -> p k t", k=KC)
          rd = SCR[:, NK + NK // 2:NK + NK // 2 + 456]
          xt = SCR[:, 0:D]
          lng = SCR[:, D:2 * D]
          lnb = SCR[:, 2 * D:3 * D]
          tA = SCR[:, 0:2 * GX].rearrange("p (c t) -> p c t", c=2)
          tB = SCR[:, 2 * GX:4 * GX].rearrange("p (c t) -> p c t", c=2)
          zc = SCR[:, 2048:2048 + 4 * G].rearrange("p (b t) -> p b t", b=4)
          gtmp = SCR[:, 4096:4096 + 2 * G].rearrange("p (b t) -> p b t", b=2)

          cnt = st[:, 0:1]
          mid = st[:, 1:2]
          tstep = st[:, 2:3]
          rmin = st[:, 3:4]
          dwid = st[:, 4:5]
          m8 = st[:, 8:16]
          ssq = st[:, 16:17]
          rstd = st[:, 17:18]
          bnst = st[:, 24:48].rearrange("p (a b) -> p a b", a=4)
          mv = st[:, 48:50]
          sd = st[:, 50:51]

          def ld(eng, dst, src, res, slot):
              return P.op(eng, lambda e: e.dma_start(out=dst, in_=src), writes=[res], dma_slot=slot)
          ld("sp", idn[:, :], idn_d[:, :], "idn", "c0")
          ld("pool", ones[:, :], ones_d[:, :], "ones", "c1")
          ld("sp", SCR[0:NT, 1024:1024 + 8 * NT], bd_d[:, :], "scr_bd", "c2")
          ld("sp", SCR[0:NT, 2048:2048 + 8 * NT], bp_d[:, :], "scr_bp", "c3")
          ld("sp", SCR[0:NT, 0:8 * NT], b31_d[:, :], "scr_b31", "c4")
          ld("pool", idnR[:, :], idn_d[:, :], "idnR", "c5")
          ld("sp", trineg[0:NT, :], trineg_d[:, :], "trineg", "c6")
          ld("sp", kflag[0:NT, :], kflag_d[0:1, :].partition_broadcast(NT), "kflag", "c7")
          ld("sp", convw[:, :, :].rearrange("p a b -> p (a b)"), convp[:, :], "convw", "c8")
          ld("sp", pscale[:, :], pscale_d[:, :], "pscale", "c9")
          ld("sp", gkb[0:NT, :], gk_d[0:1, :].partition_broadcast(NT), "gkb", "c10")
          ld("sp", fixc[:, :], fix_d[:, :], "fixc", "c11")
          ld("sp", ln1c[:, :], ln1c_d[:, :], "ln1c", "c14")
          ld("pool", wwi[:, :, :].rearrange("p a b -> p (a b)"), win_w[:, :], "wwi", "c12")
          ld("pool", wk.rearrange("p a b -> p (a b)") if False else REG[:, 4096:4096 + 16 * 320], win_k[:, :], "wk", "c13")
          P.op("pool", lambda e: e.memset(hmask[:, :], 0.0), writes=["hmask"])
          P.op("pool", lambda e: e.memset(hmask[0:64, 0:1], 1.0), writes=["hmask"])
          P.op("pool", lambda e: e.memset(hmask[64:128, 1:2], 1.0), writes=["hmask"])
          for n in range(NIT + 2):
              P.op("pool", lambda e, n=n: e.memset(pw[:, n:n + 1], 2.0 ** (-n)), writes=["pw"])
          P.op("dve", lambda e: e.tensor_copy(out=idnB[:, :], in_=idn[:, :]), reads=["idn"], writes=["idnB"])
          for (c0, Bt, nm, sn) in ((1024, Bd, "Bd", "scr_bd"), (2048, Bp, "Bp", "scr_bp")):
              P.op("dve", lambda e, c0=c0: e.tensor_tensor(out=SCR[0:NT, c0:c0 + 8 * NT], in0=SCR[0:NT, c0:c0 + 8 * NT],
                                                           in1=SCR[0:NT, 0:8 * NT], op=ALU.subtract),
                   reads=["scr_b31"], writes=[sn])
              P.op("dve", lambda e, c0=c0, Bt=Bt: e.tensor_scalar(out=Bt[0:NT].rearrange("p h t -> p (h t)"),
                                                                  in0=SCR[0:NT, c0:c0 + 8 * NT], scalar1=128.0 ** 0.5,
                                                                  scalar2=None, op0=ALU.mult),
                   reads=[sn], writes=[nm])
          P.retire(["scr_b31", "scr_bd", "scr_bp"], ["xt"])
          stop(1, [("Bd", fr(Bd[0:NT]), "Bd"), ("Bp", fr(Bp[0:NT]), "Bp")])
          def transposes_to(ps_list, srcs, rows, cols, reads, tag):
              toks = []
              nb = (len(srcs) + 3) // 4
              for b in range(nb):
                  sub = srcs[4 * b:4 * b + 4]

                  def f(e, sub=sub, b=b):
                      last = None
                      for i, s in enumerate(sub):
                          last = e.transpose(ps_list[b][0:cols, i * rows:(i + 1) * rows], s, idn[0:rows, 0:rows])
                      return last
                  P.op("pe", f, reads=list(reads) + ["idn"], writes=[tag + str(b)])
              return nb

          for kc in range(KC):
              xb = kc % 2
              xtk = xt if xb == 0 else SCR[:, 2048:4096]
              xtn = "xt" if xb == 0 else "xtB"
              P.op("sp", lambda e, kc=kc, xtk=xtk: e.dma_start(out=xtk[0:NT, :], in_=xk[kc * NT:(kc + 1) * NT, :]),
                   writes=[xtn], dma_slot=xtn)
              srcs = [xtk[0:NT, m * 128:(m + 1) * 128] for m in range(16)]
              for b in range(4):
                  def f(e, b=b, srcs=srcs):
                      last = None
                      for i in range(4):
                          last = e.transpose(PS[b][:, i * NT:(i + 1) * NT], srcs[4 * b + i], idn[0:NT, 0:NT])
                      return last
                  P.op("pe", f, reads=[xtn, "idn"], writes=["ps%d" % b])
                  P.op("act" if b % 2 == 0 else "dve",
                       (lambda e, b=b, xb=xb: e.activation(out=xT1[:, xb, 4 * b:4 * b + 4, :],
                                                           in_=PS[b][:, 0:4 * NT].rearrange("p (k t) -> p k t", k=4),
                                                           func=AF.Copy)) if b % 2 == 0 else
                       (lambda e, b=b, xb=xb: e.tensor_copy(out=xT1[:, xb, 4 * b:4 * b + 4, :],
                                                            in_=PS[b][:, 0:4 * NT].rearrange("p (k t) -> p k t", k=4))),
                       reads=["ps%d" % b], writes=["xT1_%d_%d" % (xb, b)])

              pm = 4 + 2 * (kc % 2)
              pmn = "ps%d" % pm
              ptn = "ps%d" % (pm + 1)
              kd = SCR[0:NT, 4096 + 128 * (kc % 2):4096 + 128 * (kc % 2) + 128]
              kdn = "kd%d" % (kc % 2)

              def fmm(e, xb=xb, pm=pm):
                  last = None
                  for k in range(16):
                      last = e.matmul(PS[pm][0:NT, 0:320], xT1[:, xb, k, :], wk[:, k, :], start=(k == 0), stop=(k == 15))
                  return last
              P.op("pe", fmm, reads=["xT1_%d_%d" % (xb, b_) for b_ in range(4)] + ["wk"], writes=[pmn])
              P.op("act", lambda e, pm=pm: e.activation(out=junk[0:NT, 0:256], in_=PS[pm][0:NT, 0:256], func=AF.Square,
                                                        accum_out=ssq[0:NT]),
                   reads=[pmn], writes=["ssq", "junk"])
              P.op("act", lambda e: e.activation(out=sd[0:NT], in_=ssq[0:NT], func=AF.Sqrt, scale=1.0 / 256.0, bias=EPS),
                   reads=["ssq"], writes=["sd"])
              P.op("dve", lambda e: e.reciprocal(out=rstd[0:NT], in_=sd[0:NT]), reads=["sd"], writes=["rstd"])
              P.op("dve", lambda e, kc=kc, pm=pm: e.scalar_tensor_tensor(out=ckv[0:NT, kc, :], in0=PS[pm][0:NT, 0:256],
                                                                         scalar=rstd[0:NT], in1=gkb[0:NT, :],
                                                                         op0=ALU.mult, op1=ALU.mult),
                   reads=[pmn, "rstd", "gkb"], writes=["ckv%d" % kc])
              P.op("act", lambda e, pm=pm, kd=kd: e.activation(out=kd.rearrange("p (a b) -> p a b", a=2),
                                                               in_=PS[pm][0:NT, 256:320].unsqueeze(1).to_broadcast([NT, 2, 64]),
                                                               func=AF.Copy),
                   reads=[pmn], writes=[kdn])

              def small_tr(kc=kc, pm=pm, kd=kd, kdn=kdn, ptn=ptn):
                  def ftr(e):
                      e.transpose(PS[pm + 1][:, 0:NT], fr(ckv[0:NT, kc, 0:128]), idn[0:NT, 0:NT])
                      e.transpose(PS[pm + 1][:, NT:2 * NT], fr(ckv[0:NT, kc, 128:256]), idn[0:NT, 0:NT])
                      return e.transpose(PS[pm + 1][:, 2 * NT:3 * NT], kd, idn[0:NT, 0:NT])
                  P.op("pe", ftr, reads=["ckv%d" % kc, kdn, "idn"], writes=[ptn])
                  P.op("act", lambda e: e.activation(out=ckvT[:, :, kc * NT:(kc + 1) * NT],
                                                     in_=PS[pm + 1][:, 0:2 * NT].rearrange("p (c t) -> p c t", c=2),
                                                     func=AF.Copy),
                       reads=[ptn], writes=["ckvT%d" % kc])
                  P.op("dve", lambda e: e.tensor_copy(out=kT2[:, kc * NT:(kc + 1) * NT], in_=PS[pm + 1][:, 2 * NT:3 * NT]),
                       reads=[ptn], writes=["kT2_%d" % kc])
              if kc > 0:
                  deferred_tr()
              deferred_tr = small_tr
          deferred_tr()
          CKV_ALL = ["ckv%d" % k for k in range(KC)]
          CKVT_ALL = ["ckvT%d" % k for k in range(KC)]
          KT2_ALL = ["kT2_%d" % k for k in range(KC)]

          stop(2, [("ckv", fr(ckv[0:NT]), r) for r in CKV_ALL[-1:]] + [("kT2", fr(kT2[:, :]), KT2_ALL[-1]), ("ckvT", fr(ckvT[:, :, :]), CKVT_ALL[-1])])
          out_toks = []
          wslot = [0]

          def stream_w(src_ap, nelem=2048, three=False):
              ns = 3 if three else 2
              s = wslot[0] % ns
              wslot[0] += 1
              res = "wb%d" % s if s < 2 else "wsm"
              dst = wb[:, s, :] if s < 2 else wsm[:, :]
              P.op("pool", lambda e: e.dma_start(out=dst[:, 0:nelem], in_=src_ap), writes=[res], dma_slot=res)
              return dst, res

          for g in range(NGR):
              ck0 = KOFF + g * G
              P.retire(["gT%d" % i for i in range(NFF)] + ["xT1_%d_%d" % (a_, b_) for a_ in range(2) for b_ in range(4)] + ["wk", "mixT"], ["xTe0", "xTe1", "xTe2", "xTe3", "qiT", "uT"])
              P.retire(["scores0", "scores1", "nmT", "rd", "lng", "lnb", "zc0", "zc1", "zc2", "zc3", "gtmp0", "gtmp1", "tA", "tB", "kd0", "kd1", "xtB"], ["xt"])
              pieces = [(ck0 - HALO, HALO, 0)] + [(ck0 + j * NT, NT, HALO + j * NT) for j in range(TPG)]
              P.retire(["xt"], ["xtB"])
              for pi, (r0, nr, c0) in enumerate(pieces):
                  xta = xt if pi % 2 == 0 else SCR[:, 2048:4096]
                  xtn = "xt" if pi % 2 == 0 else "xtB"
                  P.op("sp", lambda e, r0=r0, nr=nr, xta=xta: e.dma_start(out=xta[0:nr, :], in_=xk[r0:r0 + nr, :]),
                       writes=[xtn], dma_slot=xtn)
                  for b in range(4):
                      def f(e, b=b, nr=nr, xta=xta):
                          last = None
                          for i in range(4):
                              m = 4 * b + i
                              last = e.transpose(PS[b][:, i * nr:(i + 1) * nr], xta[0:nr, m * 128:(m + 1) * 128],
                                                 idn[0:nr, 0:nr])
                          return last
                      P.op("pe", f, reads=[xtn, "idn"], writes=["ps%d" % b])
                      if b % 2 == 0:
                          P.op("act", lambda e, b=b, nr=nr, c0=c0: e.activation(
                              out=xTe[:, 4 * b:4 * b + 4, c0:c0 + nr],
                              in_=PS[b][:, 0:4 * nr].rearrange("p (k t) -> p k t", k=4), func=AF.Copy),
                              reads=["ps%d" % b], writes=["xTe%d" % b])
                      else:
                          P.op("dve", lambda e, b=b, nr=nr, c0=c0: e.tensor_copy(
                              out=xTe[:, 4 * b:4 * b + 4, c0:c0 + nr],
                              in_=PS[b][:, 0:4 * nr].rearrange("p (k t) -> p k t", k=4)),
                              reads=["ps%d" % b], writes=["xTe%d" % b])
              wukS = REG[:, 11328:11328 + 2048]
              P.retire(["gT%d" % i for i in range(NFF)] + ["wk", "diagW", "qz"], ["wukS"])
              P.op("pool", lambda e: e.dma_start(out=wukS, in_=wukT[:, :]), writes=["wukS"], dma_slot="wukS")
              for j in range(TPG):
                  def fw(e, j=j):
                      last = None
                      for k in range(16):
                          last = e.matmul(PS[6][0:NT, 0:16], xTe[:, k, HALO + j * NT:HALO + (j + 1) * NT], wwi[:, k, :],
                                          start=(k == 0), stop=(k == 15))
                      return last
                  P.op("pe", fw, reads=["xTe0", "xTe1", "xTe2", "xTe3"] + ["wwi"], writes=["ps6"])
                  P.op("dve", lambda e, j=j: e.tensor_copy(out=widx[0:NT, j, :], in_=PS[6][0:NT, 0:16]),
                       reads=["ps6"], writes=["widx"])
              P.retire(["cat"], ["qabs"])
              for ci in range(24):
                  s, res = stream_w(win_m[ci, :, :], three=True)
                  pb = 4 + (ci % 2)
                  isu = 8 <= ci < 16
                  n0 = 0 if isu else HALO

                  def fm(e, s=s, pb=pb, n0=n0):
                      last = None
                      for k in range(16):
                          last = e.matmul(PS[pb][:, 0:GX - n0], s[:, k * 128:(k + 1) * 128], xTe[:, k, n0:GX],
                                          start=(k == 0), stop=(k == 15))
                      return last
                  P.op("pe", fm, reads=[res] + ["xTe0", "xTe1", "xTe2", "xTe3"], writes=["ps%d" % pb])
                  if ci < 8:
                      P.op("act", lambda e, ci=ci, pb=pb: e.activation(out=qiT[:, ci, :], in_=PS[pb][:, 0:G], func=AF.Copy),
                           reads=["ps%d" % pb], writes=["qiT"])
                  elif isu:
                      P.op("act", lambda e, ci=ci, pb=pb: e.activation(out=uT[:, ci - 8, :], in_=PS[pb][:, 0:GX], func=AF.Copy),
                           reads=["ps%d" % pb], writes=["uT"])
                  else:
                      h = ci - 16
                      qb = h % 2
                      P.op("act", lambda e, pb=pb, qb=qb: e.activation(out=qtmp[:, qb, :], in_=PS[pb][:, 0:G], func=AF.Copy),
                           reads=["ps%d" % pb], writes=["qtmp%d" % qb])
                      for cc in range(2):
                          P.op("pe", lambda e, h=h, cc=cc, qb=qb: e.matmul(
                              PS[6 + cc][:, 0:G], wukS[:, h * 256 + cc * 128:h * 256 + (cc + 1) * 128], qtmp[:, qb, :],
                              start=True, stop=True),
                              reads=["wukS", "qtmp%d" % qb], writes=["ps%d" % (6 + cc)])
                          P.op("dve", lambda e, h=h, cc=cc: e.tensor_copy(out=qabs[:, cc, h, :], in_=PS[6 + cc][:, 0:G]),
                               reads=["ps%d" % (6 + cc)], writes=["qabs"])
              if g == 0:
                  stop(3, [("qabs", fr(qabs), "qabs"), ("qiT", fr(qiT), "qiT"), ("uT", fr(uT), "uT"), ("widx", widx[0:NT], "widx")])
              P.retire(["xt", "xtB", "lng", "lnb"], ["scores0", "scores1", "nmT", "rd"])
              P.retire(["xTe0", "xTe1", "xTe2", "xTe3"], ["olat"])
              P.retire(["gT%d" % i for i in range(NFF)] + ["wk", "wukS"], ["diagW", "qz"])

              def front_ops(j):
                  J = g * TPG + j
                  scores = scb[j % 2]
                  SC = "scores%d" % (j % 2)
                  nch = 11 + J
                  nk = nch * NT
                  qc0 = j * NT
                  dcol = (nch - 1) * NT
                  nbk = (nk + 455) // 456
                  itc = [0]

                  diagW = REG[0:NT, 11328:11328 + 16 * NT].rearrange("p (h q) -> p h q", h=16)
                  qz = REG[:, 11328 + 16 * NT:11328 + 32 * NT].rearrange("p (c t q) -> p c t q", c=8, t=2)
                  seq = [(bk, h) for bk in range(nbk) for h in range(16)]

                  def dots(i):
                      bk, h = seq[i]
                      hc, hp = h // 2, (h % 2) * 64
                      c0 = bk * 456
                      w = min(456, nk - c0)
                      pb = i % 2
                      P.op("pe", lambda e: e.matmul(
                          PS[5 + pb][0:NT, 0:w], qz[:, hc, h % 2, :], kT2[:, c0:c0 + w], start=True, stop=True),
                          reads=["qz"] + KT2_ALL[c0 // NT:(c0 + w) // NT], writes=["ps%d" % (5 + pb)])
                      P.op("act", lambda e: e.activation(out=rl[0:NT, pb, 0:w], in_=PS[5 + pb][0:NT, 0:w], func=AF.Relu),
                           reads=["ps%d" % (5 + pb)], writes=["rl%d" % pb])

                  def accum(i):
                      bk, h = seq[i]
                      c0 = bk * 456
                      w = min(456, nk - c0)
                      pb = i % 2
                      P.op("pe", lambda e: e.matmul(PS[7][0:NT, 0:w], diagW[:, h, :], rl[0:NT, pb, 0:w],
                                                    start=(h == 0), stop=(h == 15)),
                           reads=["rl%d" % pb, "diagW"], writes=["ps7"])
                      if h == 15:
                          P.op("act", lambda e: e.activation(out=scores[0:NT, c0:c0 + w], in_=PS[7][0:NT, 0:w], func=AF.Copy),
                               reads=["ps7"], writes=[SC])

                  def item(i):
                      if i == 0:
                          P.op("dve", lambda e: e.tensor_tensor(
                              out=diagW, in0=idn[0:NT, 0:NT].unsqueeze(1).to_broadcast([NT, 16, NT]),
                              in1=widx[0:NT, j, :].unsqueeze(2).to_broadcast([NT, 16, NT]), op=ALU.mult),
                              reads=["idn", "widx"], writes=["diagW"])
                          P.op("dve", lambda e: e.tensor_tensor(
                              out=qz, in0=fr(qiT[:, :, qc0:qc0 + NT]).unsqueeze(2).to_broadcast([128, 8, 2, NT]),
                              in1=hmask[:, :].unsqueeze(1).unsqueeze(3).to_broadcast([128, 8, 2, NT]), op=ALU.mult),
                              reads=["qiT", "hmask"], writes=["qz"])
                          dots(0)
                      if i + 1 < len(seq):
                          dots(i + 1)
                      accum(i)
                  items = [(lambda i=i: item(i)) for i in range(len(seq))]

                  Bl, Tl = [], []

                  def build_tail():
                      OP = lambda *a, **k: Bl.append((a, k))
                      OT = lambda *a, **k: Tl.append((a, k))
                      OP("dve", lambda e: e.max(out=m8[0:NT], in_=scores[0:NT, 0:nk]), reads=[SC], writes=["m8"])
                      OP("dve", lambda e: e.tensor_reduce(out=rmin[0:NT], in_=scores[0:NT, 0:nk], op=ALU.min, axis=AX.X),
                           reads=[SC], writes=["rmin"])
                      OP("dve", lambda e: e.tensor_tensor(out=dwid[0:NT], in0=m8[0:NT, 0:1], in1=rmin[0:NT], op=ALU.subtract),
                           reads=["m8", "rmin"], writes=["dwid"])
                      OP("dve", lambda e: e.tensor_scalar(out=dwid[0:NT], in0=dwid[0:NT], scalar1=1.002, scalar2=2e-6,
                                                            op0=ALU.mult, op1=ALU.add), reads=["dwid"], writes=["dwid"])
                      OP("dve", lambda e: e.tensor_scalar(out=wcols[0:NT, :], in0=pw[0:NT, :], scalar1=dwid[0:NT], scalar2=None,
                                                            op0=ALU.mult), reads=["pw", "dwid"], writes=["wcols"])
                      OP("dve", lambda e: e.scalar_tensor_tensor(out=mid[0:NT], in0=dwid[0:NT], scalar=-0.001, in1=rmin[0:NT],
                                                                   op0=ALU.mult, op1=ALU.add),
                           reads=["rmin", "dwid"], writes=["mid"])
                      OP("dve", lambda e: e.tensor_tensor(out=mid[0:NT], in0=mid[0:NT], in1=wcols[0:NT, 1:2], op=ALU.add),
                           reads=["wcols"], writes=["mid"])
                      OP("dve", lambda e: e.scalar_tensor_tensor(out=scores[0:NT, 0:KOFF], in0=kflag[0:NT, :], scalar=NEG,
                                                                   in1=scores[0:NT, 0:KOFF], op0=ALU.mult, op1=ALU.add),
                           reads=["kflag", "rmin", "m8"], writes=[SC])
                      OP("dve", lambda e: e.tensor_tensor(out=scores[0:NT, dcol:dcol + NT], in0=scores[0:NT, dcol:dcol + NT],
                                                            in1=trineg[0:NT, :], op=ALU.add),
                           reads=["trineg"], writes=[SC])
                      for n in range(1, NIT + 1):
                          OP("dve", lambda e: e.tensor_scalar(out=junk[0:NT, 0:nk], in0=scores[0:NT, 0:nk], scalar1=mid[0:NT],
                                                                scalar2=None, op0=ALU.is_ge, op1=ALU.add, accum_out=cnt[0:NT]),
                               reads=[SC, "mid"], writes=["cnt", "junk"])
                          OP("dve", lambda e, n=n: e.tensor_scalar(out=tstep[0:NT], in0=cnt[0:NT], scalar1=255.5,
                                                                     scalar2=wcols[0:NT, n:n + 1], op0=ALU.is_ge, op1=ALU.mult),
                               reads=["cnt", "wcols"], writes=["tstep"])
                          nn = n + 1 if n < NIT else n
                          OP("dve", lambda e, nn=nn: e.scalar_tensor_tensor(out=mid[0:NT], in0=tstep[0:NT], scalar=wcols[0:NT, nn:nn + 1],
                                                                              in1=mid[0:NT], op0=ALU.subtract, op1=ALU.add),
                               reads=["tstep", "wcols"], writes=["mid"])
                      OP("dve", lambda e: e.tensor_scalar(out=scores[0:NT, 0:nk], in0=scores[0:NT, 0:nk], scalar1=mid[0:NT],
                                                            scalar2=None, op0=ALU.is_ge),
                           reads=["mid"], writes=[SC])
                      nb4 = (nch + 3) // 4
                      for b in range(nb4):
                          k0 = 4 * b
                          k1 = min(nch, k0 + 4)

                          def ft(e, k0=k0, k1=k1, b=b):
                              last = None
                              for i, kc in enumerate(range(k0, k1)):
                                  last = e.transpose(PS[5 + (b % 2)][0:NT, i * NT:(i + 1) * NT], scores[0:NT, kc * NT:(kc + 1) * NT],
                                                     idn[0:NT, 0:NT])
                              return last
                          OT("pe", ft, reads=[SC, "idn"], writes=["ps%d" % (5 + b % 2)])
                          OT("act", lambda e, k0=k0, k1=k1, b=b: e.activation(
                              out=nmT[0:NT, k0:k1, :], in_=PS[5 + (b % 2)][0:NT, 0:(k1 - k0) * NT].rearrange("p (k t) -> p k t", k=k1 - k0),
                              func=AF.Identity, scale=MBIG, bias=-MBIG), reads=["ps%d" % (5 + b % 2)], writes=["nmT"])
                  build_tail()
                  return items, Bl, Tl

              def back_ops(j):
                  J = g * TPG + j
                  nch = 11 + J
                  qc0 = j * NT
                  its = []
                  cnt_it = [0]

                  def chunk_l(hh, kc, pb):
                      def fl(e):
                          near = kc >= nch - 2
                          e.matmul(PS[pb][0:NT, 0:456], ckvT[:, 0, kc * NT:(kc + 1) * NT],
                                   qabs[:, 0, 4 * hh:4 * hh + 4, qc0:qc0 + NT], start=True, stop=False)
                          e.matmul(PS[pb][0:NT, 0:456], ckvT[:, 1, kc * NT:(kc + 1) * NT],
                                   qabs[:, 1, 4 * hh:4 * hh + 4, qc0:qc0 + NT], start=False, stop=False)
                          if near:
                              Bt = Bd if kc == nch - 1 else Bp
                              e.matmul(PS[pb][0:NT, 0:456], idnR[0:NT, 0:NT], Bt[0:NT, 4 * hh:4 * hh + 4, :], start=False, stop=False)
                          return e.matmul(PS[pb][0:NT, 0:456], idnB[0:NT, 0:NT],
                                          nmT[0:NT, kc, :].unsqueeze(1).to_broadcast([NT, 4, NT]), start=False, stop=True)
                      P.op("pe", fl, reads=["ckvT%d" % kc, "qabs", "nmT", "Bd", "Bp", "idnR", "idnB"], writes=["ps%d" % pb])
                      P.op("act", lambda e: e.activation(out=em[0:NT, pb, :], in_=PS[pb][0:NT, 0:456], func=AF.Exp,
                                                         scale=128.0 ** -0.5),
                           reads=["ps%d" % pb], writes=["em%d" % pb])

                  def chunk_pv(hh, kc, pb):
                      def fpv(e):
                          st_, sp_ = (kc == 0), (kc == nch - 1)
                          e.matmul(PS[2][:, 0:456], ckv[0:NT, kc, 0:128], em[0:NT, pb, :], start=st_, stop=sp_)
                          e.matmul(PS[3][:, 0:456], ckv[0:NT, kc, 128:256], em[0:NT, pb, :], start=st_, stop=sp_)
                          return e.matmul(PS[4][:, 0:456], ones[0:NT, :], em[0:NT, pb, :], start=st_, stop=sp_)
                      P.op("pe", fpv, reads=["ckv%d" % kc, "em%d" % pb, "ones"], writes=["ps2", "ps3", "ps4"])
                      if kc == nch - 1:
                          P.op("dve", lambda e: e.reciprocal(out=rd[:, :], in_=PS[4][:, 0:456]), reads=["ps4"], writes=["rd"])
                          for cc in range(2):
                              P.op("dve", lambda e, cc=cc: e.tensor_tensor(
                                  out=olat[:, cc, 4 * hh:4 * hh + 4, qc0:qc0 + NT],
                                  in0=PS[2 + cc][:, 0:456].rearrange("p (h t) -> p h t", h=4),
                                  in1=rd[:, :].rearrange("p (h t) -> p h t", h=4), op=ALU.mult),
                                  reads=["ps%d" % (2 + cc), "rd"], writes=["olat"])
                  seq = [(hh, kc, i % 2) for i, (hh, kc) in enumerate((hh, kc) for hh in range(2) for kc in range(nch))]
                  for i, (hh, kc, pb) in enumerate(seq):
                      def it(i=i):
                          if i == 0:
                              chunk_l(*seq[0])
                          if i + 1 < len(seq):
                              chunk_l(*seq[i + 1])
                          chunk_pv(*seq[i])
                      its.append(it)
                  return its

              fr_ = [front_ops(j) for j in range(TPG)]
              bk_ = [None] * TPG

              def emit_all(ops):
                  for a, k in ops:
                      P.op(*a, **k)
              for t in range(TPG + 2):
                  Il = fr_[t][0] if t < TPG else []
                  Bl = fr_[t - 1][1] if 0 <= t - 1 < TPG else []
                  Tl = fr_[t - 1][2] if 0 <= t - 1 < TPG else []
                  if 0 <= t - 2 < TPG:
                      bk_[t - 2] = back_ops(t - 2)
                      Kl = bk_[t - 2]
                  else:
                      Kl = []
                  lead = Il if Il else (Kl if Kl else [None])
                  n = len(lead)
                  dk = db = 0
                  for i in range(n):
                      if Il:
                          Il[i]()
                      wk_ = (len(Kl) * (i + 1)) // n
                      while dk < wk_:
                          Kl[dk]()
                          dk += 1
                      wb_ = (len(Bl) * (i + 1)) // n
                      while db < wb_:
                          a, k = Bl[db]
                          P.op(*a, **k)
                          db += 1
                  emit_all(Tl)
                  if g == 0 and t == 1:
                      stop(4, [("mask", scb[0][0:NT, 0:11 * NT], "scores0"), ("thr", mid[0:NT], "mid")])
              if g == 0:
                  stop(5, [("olat", fr(olat), "olat")])
              P.op("pool", lambda e: e.dma_start(out=wsm[:, :], in_=wuv[:, :]), writes=["wsm"], dma_slot="wsm")
              P.retire(["qabs"], ["cat"])
              for h in range(8):
                  pb = 6 + (h % 2)

                  def fa(e, h=h, pb=pb):
                      e.matmul(PS[pb][:, 0:G], wsm[:, (0 * 8 + h) * 128:(0 * 8 + h + 1) * 128], olat[:, 0, h, :], start=True, stop=False)
                      return e.matmul(PS[pb][:, 0:G], wsm[:, (1 * 8 + h) * 128:(1 * 8 + h + 1) * 128], olat[:, 1, h, :],
                                      start=False, stop=True)
                  P.op("pe", fa, reads=["wsm", "olat"], writes=["ps%d" % pb])
                  P.op("act", lambda e, h=h, pb=pb: e.activation(out=cat[:, h, :], in_=PS[pb][:, 0:G], func=AF.Copy),
                       reads=["ps%d" % pb], writes=["cat"])
              P.op("pool", lambda e: e.dma_start(out=wsm[:, :], in_=wpool[:, :]), writes=["wsm"], dma_slot="wsm")
              P.retire(["scores0", "scores1", "nmT", "rd"], ["tA", "tB"])
              for gi in range(4):
                  cur = fr(uT[:, 2 * gi:2 * gi + 2, :])
                  bufs = [tA, tB]
                  for lev in range(gi + 1):
                      sh = 1 << lev
                      nxt = bufs[lev % 2]
                      vs = sh - 1
                      P.op("dve", lambda e, cur=cur, nxt=nxt, sh=sh, vs=vs: e.tensor_tensor(
                          out=nxt[:, :, vs + sh:GX], in0=cur[:, :, vs + sh:GX], in1=cur[:, :, vs:GX - sh], op=ALU.add),
                          reads=["uT", "tA", "tB"], writes=["tA" if lev % 2 == 0 else "tB"])
                      cur = nxt
                  wn = "tA" if gi % 2 == 0 else "tB"
                  if gi == 3 and g == 0:
                      P.op("dve", lambda e, cur=cur: e.tensor_scalar(out=cur[:, :, HALO:HALO + 1], in0=cur[:, :, HALO:HALO + 1],
                                                                     scalar1=fixc[:, 0:1], scalar2=None, op0=ALU.mult),
                           reads=["fixc"], writes=[wn])
                  P.op("dve", lambda e, cur=cur, gi=gi: e.scalar_tensor_tensor(
                      out=dT[:, :, :], in0=cur[:, :, HALO:GX], scalar=1.0 / (2 << gi), in1=fr(uT[:, 2 * gi:2 * gi + 2, HALO:GX]),
                      op0=ALU.mult, op1=ALU.subtract), reads=[wn, "uT"], writes=["dT"])
                  for oc in range(2):
                      pb = 6 + oc

                      def fp(e, gi=gi, oc=oc, pb=pb):
                          b0 = (gi * 2 + 0) * 256 + oc * 128
                          b1 = (gi * 2 + 1) * 256 + oc * 128
                          e.matmul(PS[pb][:, 0:G], wsm[:, b0:b0 + 128], dT[:, 0, :], start=True, stop=False)
                          return e.matmul(PS[pb][:, 0:G], wsm[:, b1:b1 + 128], dT[:, 1, :], start=False, stop=True)
                      P.op("pe", fp, reads=["wsm", "dT"], writes=["ps%d" % pb])
                      P.op("act", lambda e, gi=gi, oc=oc, pb=pb: e.activation(
                          out=cat[:, 8 + gi * 2 + oc, :], in_=PS[pb][:, 0:G], func=AF.Identity,
                          scale=pscale[:, gi * 2 + oc:gi * 2 + oc + 1]),
                          reads=["ps%d" % pb, "pscale"], writes=["cat"])
              if g == 0:
                  stop(6, [("cat", fr(cat[:, :, :]), "cat")])
              P.retire(["olat"] + ["xTe0", "xTe1", "xTe2", "xTe3"], ["mixT"])
              for m in range(16):
                  s, res = stream_w(wo[m, :, :], three=True)
                  pb = 4 + (m % 2)

                  def fo(e, s=s, pb=pb):
                      last = None
                      for k in range(16):
                          last = e.matmul(PS[pb][:, 0:G], s[:, k * 128:(k + 1) * 128], cat[:, k, :],
                                          start=(k == 0), stop=(k == 15))
                      return last
                  P.op("pe", fo, reads=[res, "cat"], writes=["ps%d" % pb])
                  P.op("act", lambda e, m=m, pb=pb: e.activation(out=mixT[:, m, :], in_=PS[pb][:, 0:G], func=AF.Copy),
                       reads=["ps%d" % pb], writes=["mixT"])
              if g == 0:
                  stop(7, [("mixT", fr(mixT), "mixT")])
              P.retire(["tA", "tB", "scores0", "scores1", "nmT", "rd"], ["xt", "lng", "lnb"])
              P.retire(["cat"], ["h1T"])

              def layer_norm_tile(src_views, resid_fn, W, wn, par, affine):
                  S = st if par == 0 else st2
                  bnst_, mv_, sd_, rstd_ = (S[:, 24:48].rearrange("p (a b) -> p a b", a=4), S[:, 48:50], S[:, 50:51], S[:, 17:18])
                  sn = "lnst%d" % par
                  for b in range(4):
                      def f(e, b=b):
                          last = None
                          for i in range(4):
                              last = e.transpose(PS[b][0:NT, i * 128:(i + 1) * 128], src_views[4 * b + i], idn[:, :])
                          return last
                      P.op("pe", f, reads=["mixT", "h1T", "idn"], writes=["ps%d" % b])
                      if resid_fn is not None:
                          resid_fn(b)
                  for b in range(4):
                      if resid_fn is not None:
                          P.op("dve", lambda e, b=b: e.bn_stats(out=bnst_[0:NT, b, :], in_=W[0:NT, b * 512:(b + 1) * 512]),
                               reads=[wn], writes=[sn])
                      else:
                          P.op("dve", lambda e, b=b: e.bn_stats(out=bnst_[0:NT, b, :], in_=PS[b][0:NT, :]),
                               reads=["ps%d" % b], writes=[sn])
                  P.op("dve", lambda e: e.bn_aggr(out=mv_[0:NT], in_=bnst_[0:NT].rearrange("p a b -> p (a b)")),
                       reads=[sn], writes=[sn])
                  P.op("act", lambda e: e.activation(out=sd_[0:NT], in_=mv_[0:NT, 1:2], func=AF.Sqrt, bias=EPS),
                       reads=[sn], writes=[sn])
                  P.op("dve", lambda e: e.reciprocal(out=rstd_[0:NT], in_=sd_[0:NT]), reads=[sn], writes=[sn])
                  nmr_ = S[:, 18:19]
                  if resid_fn is None:
                      P.op("dve", lambda e: e.scalar_tensor_tensor(out=nmr_[0:NT], in0=mv_[0:NT, 0:1], scalar=-1.0, in1=rstd_[0:NT],
                                                                   op0=ALU.mult, op1=ALU.mult), reads=[sn], writes=[sn])
                  if resid_fn is not None:
                      P.op("dve", lambda e: e.tensor_scalar(out=W[0:NT, :], in0=W[0:NT, :], scalar1=mv_[0:NT, 0:1], scalar2=rstd_[0:NT],
                                                            op0=ALU.subtract, op1=ALU.mult),
                           reads=[sn], writes=[wn])
                  else:
                      for b in range(4):
                          P.op("dve" if b % 2 == 0 else "act",
                               (lambda e, b=b: e.tensor_scalar(out=W[0:NT, b * 512:(b + 1) * 512], in0=PS[b][0:NT, :],
                                                               scalar1=mv_[0:NT, 0:1], scalar2=rstd_[0:NT],
                                                               op0=ALU.subtract, op1=ALU.mult)) if b % 2 == 0 else
                               (lambda e, b=b: e.activation(out=W[0:NT, b * 512:(b + 1) * 512], in_=PS[b][0:NT, :],
                                                            func=AF.Identity, scale=rstd_[0:NT], bias=nmr_[0:NT])),
                               reads=[sn, "ps%d" % b], writes=[wn + "_%d" % b])
                  if affine:
                      for b in range(4):
                          P.op("dve", lambda e, b=b: e.tensor_tensor(out=W[0:NT, b * 512:(b + 1) * 512], in0=W[0:NT, b * 512:(b + 1) * 512],
                                                                     in1=lng[0:NT, b * 512:(b + 1) * 512], op=ALU.mult),
                               reads=["lng"], writes=[wn + "_%d" % b])
                          P.op("pool" if b % 2 == 0 else "dve", lambda e, b=b: e.tensor_tensor(
                              out=W[0:NT, b * 512:(b + 1) * 512], in0=W[0:NT, b * 512:(b + 1) * 512],
                              in1=lnb[0:NT, b * 512:(b + 1) * 512], op=ALU.add),
                              reads=["lnb"], writes=[wn + "_%d" % b])

              LW1 = [(xt, "xt"), (lng, "lng"), (lnb, "lnb")]
              for j in range(TPG):
                  r0 = ck0 + j * NT
                  W, wn = LW1[j]
                  P.op("sp", lambda e, r0=r0, W=W: e.dma_start(out=W[0:NT, :], in_=xk[r0:r0 + NT, :]), writes=[wn], dma_slot=wn)
              for j in range(TPG):
                  W, wn = LW1[j]

                  def resid1(b, W=W, wn=wn):
                      P.op("dve", lambda e, b=b: e.scalar_tensor_tensor(
                          out=W[0:NT, b * 512:(b + 1) * 512], in0=W[0:NT, b * 512:(b + 1) * 512], scalar=ALPHA,
                          in1=PS[b][0:NT, :], op0=ALU.mult, op1=ALU.add), reads=["ps%d" % b], writes=[wn])
                  layer_norm_tile([fr(mixT[:, m, j * NT:(j + 1) * NT]) for m in range(16)], resid1, W, wn, j % 2, False)
                  for b in range(4):
                      def f(e, b=b, W=W):
                          last = None
                          for i in range(4):
                              m = 4 * b + i
                              last = e.transpose(PS[4 + (b % 2)][:, i * NT:(i + 1) * NT], W[0:NT, m * 128:(m + 1) * 128],
                                                 idn[0:NT, 0:NT])
                          return last
                      P.op("pe", f, reads=[wn, "idn"], writes=["ps%d" % (4 + b % 2)])
                      for i in range(4):
                          m = 4 * b + i
                          P.op("act", lambda e, b=b, i=i, m=m, j=j: e.activation(
                              out=cat[:, m, j * NT:(j + 1) * NT], in_=PS[4 + (b % 2)][:, i * NT:(i + 1) * NT],
                              func=AF.Identity, scale=ln1c[:, m:m + 1], bias=ln1c[:, 16 + m:17 + m]),
                              reads=["ps%d" % (4 + b % 2), "ln1c"], writes=["h1T"])
              if g == 0:
                  stop(8, [("h1T", fr(cat[:, :, :]), "h1T")])
              P.retire(["mixT", "qiT", "uT", "olat"] + ["xTe0", "xTe1", "xTe2", "xTe3"], ["gT%d" % i for i in range(NFF)])
              P.retire(["xt", "lng", "lnb", "tA", "tB", "scores0", "scores1", "nmT", "rd"], ["zc0", "zc1", "zc2", "zc3", "gtmp0", "gtmp1"])
              for i in range(NFF):
                  for half in range(2):
                      cidx = i + NFF * half
                      s, res = stream_w(wup[cidx, :, :], three=True)
                      pb = 4 + 2 * (i % 2) + half
                      zb = 2 * (i % 2) + half

                      def fu(e, s=s, pb=pb):
                          last = None
                          for k in range(16):
                              last = e.matmul(PS[pb][:, 0:G], s[:, k * 128:(k + 1) * 128], cat[:, k, :],
                                              start=(k == 0), stop=(k == 15))
                          return last
                      P.op("pe", fu, reads=[res, "h1T"], writes=["ps%d" % pb])
                      zr = "zc%d" % zb
                      P.op("act", lambda e, cidx=cidx, pb=pb, zb=zb: e.activation(
                          out=zc[:, zb, :], in_=PS[pb][:, 0:G], func=AF.Identity, scale=convw[:, cidx, 2:3], bias=convw[:, cidx, 3:4]),
                          reads=["ps%d" % pb, "convw"], writes=[zr])
                      P.op("dve", lambda e, cidx=cidx, pb=pb, zb=zb: e.scalar_tensor_tensor(
                          out=zc[:, zb, 1:G], in0=PS[pb][:, 0:G - 1], scalar=convw[:, cidx, 1:2], in1=zc[:, zb, 1:G],
                          op0=ALU.mult, op1=ALU.add), reads=["ps%d" % pb, "convw"], writes=[zr])
                      P.op("dve", lambda e, cidx=cidx, pb=pb, zb=zb: e.scalar_tensor_tensor(
                          out=zc[:, zb, 2:G], in0=PS[pb][:, 0:G - 2], scalar=convw[:, cidx, 0:1], in1=zc[:, zb, 2:G],
                          op0=ALU.mult, op1=ALU.add), reads=["ps%d" % pb, "convw"], writes=[zr])
                      if g > 0:
                          P.op("dve", lambda e, cidx=cidx, zb=zb: e.scalar_tensor_tensor(
                              out=zc[:, zb, 0:2], in0=halo[:, cidx, 0:2], scalar=convw[:, cidx, 0:1], in1=zc[:, zb, 0:2],
                              op0=ALU.mult, op1=ALU.add), reads=["halo%d" % cidx, "convw"], writes=[zr])
                          P.op("dve", lambda e, cidx=cidx, zb=zb: e.scalar_tensor_tensor(
                              out=zc[:, zb, 0:1], in0=halo[:, cidx, 1:2], scalar=convw[:, cidx, 1:2], in1=zc[:, zb, 0:1],
                              op0=ALU.mult, op1=ALU.add), reads=["halo%d" % cidx, "convw"], writes=[zr])
                      P.op("act", lambda e, cidx=cidx, pb=pb: e.activation(out=halo[:, cidx, :], in_=PS[pb][:, G - 2:G], func=AF.Copy),
                           reads=["ps%d" % pb], writes=["halo%d" % cidx])
                  za, zg = 2 * (i % 2), 2 * (i % 2) + 1
                  gb = i % 2
                  P.op("act", lambda e, za=za, gb=gb: e.activation(out=gtmp[:, gb, :], in_=zc[:, za, :], func=AF.Gelu_apprx_tanh),
                       reads=["zc%d" % za], writes=["gtmp%d" % gb])
                  P.op("dve", lambda e, i=i, zg=zg, gb=gb: e.tensor_tensor(out=gT[:, i, :], in0=gtmp[:, gb, :], in1=zc[:, zg, :], op=ALU.mult),
                       reads=["gtmp%d" % gb, "zc%d" % zg], writes=["gT%d" % i])
              if g == 0:
                  stop(9, [("gT", fr(gT[:, :, :]), "gT%d" % (NFF - 1))])
              for m in range(16):
                  pb = 4 + (m % 2)
                  for (k0, nkk) in ((0, 16), (16, 16), (32, 12)):
                      s, res = stream_w(wdn[m, :, k0 * 128:(k0 + nkk) * 128], nelem=nkk * 128, three=True)

                      def fd(e, s=s, pb=pb, k0=k0, nkk=nkk):
                          last = None
                          for k in range(nkk):
                              kk = k0 + k
                              last = e.matmul(PS[pb][:, 0:G], s[:, k * 128:(k + 1) * 128], gT[:, kk, :],
                                              start=(kk == 0), stop=(kk == NFF - 1))
                          return last
                      P.op("pe", fd, reads=[res] + ["gT%d" % (k0 + k) for k in range(nkk)], writes=["ps%d" % pb])
                  P.op("dve", lambda e, m=m, pb=pb: e.scalar_tensor_tensor(
                      out=cat[:, m, :], in0=fr(cat[:, m, :]), scalar=ALPHA, in1=PS[pb][:, 0:G], op0=ALU.mult, op1=ALU.add),
                      reads=["ps%d" % pb], writes=["h1T"])
              if g == 0:
                  stop(10, [("h2T", fr(cat[:, :, :]), "h1T")])
              P.retire(["zc0", "zc1", "zc2", "zc3", "gtmp0", "gtmp1"], ["xt", "lng", "lnb"])
              P.op("sp", lambda e: e.dma_start(out=lng[0:NT, :], in_=ln_d[2:3, :].partition_broadcast(NT)), writes=["lng"], dma_slot="lng")
              P.op("sp", lambda e: e.dma_start(out=lnb[0:NT, :], in_=ln_d[3:4, :].partition_broadcast(NT)), writes=["lnb"], dma_slot="lnb")
              LW2 = [(xt, "xt"), (xt, "xt")]
              for j in range(TPG):
                  J = g * TPG + j
                  W, wn = LW2[J % 2]

                  XB = [wn + "_%d" % b for b in range(4)]
                  P.retire([wn], XB)
                  layer_norm_tile([fr(cat[:, m, j * NT:(j + 1) * NT]) for m in range(16)], None, W, wn, J % 2, True)
                  if J == 0:
                      t = P.op("sp", lambda e, W=W: e.dma_start(out=out[0:NT - 2, :], in_=W[2:NT, :]), reads=XB, dma_slot="ost%d" % (J % 2))
                  else:
                      o0 = J * NT - 2
                      t = P.op("sp", lambda e, o0=o0, W=W: e.dma_start(out=out[o0:o0 + NT, :], in_=W[0:NT, :]), reads=XB,
                               dma_slot="ost%d" % (J % 2))
                  P.retire(XB, [wn])
                  out_toks.append(t)
              P.retire(["h1T"], ["cat"])
          P.final_wait("sp", out_toks[-2:])
          P.emit()
      except _Done:
        pass
    nc._dbg_dumps = dumps
    return nc


def t5_bucket_np(dist):
    max_exact = 16
    d_f = np.maximum(dist, 1).astype(np.float32)
    large = max_exact + (np.log(d_f / max_exact) / math.log(128 / max_exact) * (32 - max_exact)).astype(np.int32)
    large = np.minimum(large, 31)
    return np.where(dist < max_exact, dist, large)


_NC_CACHE = {}


def prep_shared(meta, rel_bias, w_in, kv_norm_g, w_uk, w_uv, w_pool, pool_scale, w_o, ln1_g, ln1_b,
                w_up, conv_w, conv_b, w_down, ln2_g, ln2_b):
    f = np.float32
    W = np.asarray(w_in[0], f)

    def blk(Wc):
        n = Wc.shape[1]
        return np.ascontiguousarray(Wc.reshape(16, 128, n).transpose(1, 0, 2)).reshape(128, 16 * n)
    cols = [C_QI + 128 * i for i in range(8)] + [C_U + 128 * i for i in range(8)] + [C_Q + 128 * i for i in range(8)]
    sh = {}
    sh["win_m"] = np.stack([blk(W[:, c:c + 128]) for c in cols])
    sh["win_k"] = blk(np.concatenate([W[:, C_KV:C_KV + 256], W[:, C_KI:C_KI + 64]], axis=1))
    sh["win_w"] = blk(W[:, C_WI:C_WI + 16])
    sh["wukT"] = np.ascontiguousarray(np.asarray(w_uk[0], f).transpose(2, 1, 0)).reshape(128, 8 * 256)
    sh["wuv"] = np.ascontiguousarray(np.asarray(w_uv[0], f).reshape(2, 128, 8, 128).transpose(1, 0, 2, 3)).reshape(128, 2048)
    sh["wpool"] = np.ascontiguousarray(np.asarray(w_pool[0], f).reshape(4, 2, 128, 256).transpose(2, 0, 1, 3)).reshape(128, 2048)
    Wo = np.asarray(w_o[0], f)
    sh["wo"] = np.stack([blk(Wo[:, 128 * m:128 * m + 128]) for m in range(16)])
    Wu = np.asarray(w_up[0], f)
    sh["wup"] = np.stack([blk(Wu[:, 128 * m:128 * m + 128]) for m in range(88)])
    Wd = np.asarray(w_down[0], f)
    wd = Wd.reshape(44, 128, 16, 128).transpose(2, 1, 0, 3)
    sh["wdn"] = np.ascontiguousarray(wd).reshape(16, 128, 44 * 128)
    cw = np.asarray(conv_w[0], f)
    cb = np.asarray(conv_b[0], f)
    cp = np.concatenate([cw, cb[None]], axis=0)
    sh["convp"] = np.ascontiguousarray(cp.reshape(4, 88, 128).transpose(2, 1, 0)).reshape(128, 88 * 4)
    sh["pscale"] = np.ascontiguousarray(np.asarray(pool_scale[0], f).reshape(8, 128).T)
    sh["gk"] = np.asarray(kv_norm_g[0], f).reshape(1, 256)
    sh["lnp"] = np.stack([np.asarray(a[0], f) for a in (ln1_g, ln1_b, ln2_g, ln2_b)])
    sh["ln1c"] = np.ascontiguousarray(np.concatenate([np.asarray(ln1_g[0], f).reshape(16, 128).T, np.asarray(ln1_b[0], f).reshape(16, 128).T], axis=1))
    sh["idn"] = np.eye(128, dtype=f)
    sh["ones"] = np.ones((128, 128), f)
    rb = np.asarray(rel_bias, f)
    p = np.arange(NT)[:, None]
    i = np.arange(NT)[None, :]
    dd = t5_bucket_np(np.maximum(i - p, 0))
    dp = t5_bucket_np(i - p + NT)
    sh["bd"] = np.ascontiguousarray(rb[dd].transpose(0, 2, 1)).reshape(NT, 8 * NT)
    sh["bp"] = np.ascontiguousarray(rb[dp].transpose(0, 2, 1)).reshape(NT, 8 * NT)
    sh["b31"] = np.ascontiguousarray(np.broadcast_to(rb[31][None, :, None], (NT, 8, NT))).reshape(NT, 8 * NT)
    sh["tri01"] = (i >= p).astype(f)
    sh["trineg"] = np.where(p >= i, 0.0, NEG).astype(f)
    return sh


def kernel(x, meta, rel_bias, w_in, kv_norm_g, w_uk, w_uv, w_pool, pool_scale, w_o, ln1_g, ln1_b,
           w_up, conv_w, conv_b, w_down, ln2_g, ln2_b):
    x = np.asarray(x, np.float32)
    meta = np.asarray(meta, np.float32)
    sh = prep_shared(meta, rel_bias, w_in, kv_norm_g, w_uk, w_uv, w_pool, pool_scale, w_o, ln1_g, ln1_b,
                     w_up, conv_w, conv_b, w_down, ln2_g, ln2_b)
    if "nc" not in _NC_CACHE:
        _NC_CACHE["nc"] = build_nc()
    nc = _NC_CACHE["nc"]
    in_maps = []
    for c in range(8):
        b, half = c // 2, c % 2
        base = 14 + 1024 * half
        hfull = np.concatenate([meta, x[b]], axis=0)
        xk = np.zeros((NK, D), np.float32)
        tok = base - KOFF + np.arange(NK)
        valid = tok >= 0
        xk[valid] = hfull[tok[valid]]
        m = dict(sh)
        m["xk"] = xk
        m["kflag"] = (~valid[:KOFF]).astype(np.uint8).reshape(1, KOFF)
        m["fixc"] = np.full((128, 1), 16.0 / 15.0 if half == 0 else 1.0, np.float32)
        in_maps.append(m)
    res = run_bass_kernel_spmd(nc, in_maps, core_ids=list(range(8)))
    outp = np.zeros((4, 2048, D), np.float32)
    for c in range(8):
        b, half = c // 2, c % 2
        outp[b, 1024 * half:1024 * (half + 1)] = res.results[c]["out"]
    return outp
```

```python
import math
import numpy as np
import concourse.bass as bass
import concourse.mybir as mybir
from concourse.bass_utils import run_bass_kernel_spmd
from contextlib import ExitStack

F32 = mybir.dt.float32
F32R = mybir.dt.float32r
U8 = mybir.dt.uint8
AF = mybir.ActivationFunctionType
ALU = mybir.AluOpType
AX = mybir.AxisListType

D = 2048
NT = 114
TPG = 3
G = NT * TPG
NGR = 3
NQT = 9
KC = 19
NK = KC * NT
KOFF = 10 * NT
HALO = 16
GX = G + HALO
DFF = 5632
NFF = 44
ALPHA = 2.0 ** 0.25
EPS = 1e-5
NIT = 18
MBIG = 30000.0
BF16 = mybir.dt.bfloat16
NEG = -1.0e30
C_Q, C_KV, C_QI, C_KI, C_WI, C_U = 0, 1024, 1280, 2304, 2368, 2384


class Prog:
    ENG = ("pe", "act", "dve", "pool", "sp")

    def __init__(self, nc, es):
        self.nc = nc
        self.es = es
        self.streams = {e: [] for e in self.ENG}
        self.count = {e: 0 for e in self.ENG}
        self.sems = {}
        for e in self.ENG:
            self.sems[("eng", e)] = es.enter_context(nc.semaphore("sem_" + e))
        self.dma_count = {}
        self.last_w = {}
        self.readers = {}
        self.waited = {e: {} for e in self.ENG}

    def _dma_sem(self, slot):
        k = ("dma", slot)
        if k not in self.sems:
            self.sems[k] = self.es.enter_context(self.nc.semaphore("dsem%d" % len(self.sems)))
            self.dma_count[slot] = 0
        return k

    def retire(self, old, new):
        toks = []
        for n in old:
            t = self.last_w.get(n)
            if t is not None:
                toks.append(t)
            toks.extend(self.readers.get(n, ()))
        for n in new:
            self.readers[n] = list(toks) + list(self.readers.get(n, ()))

    def op(self, eng, fn, reads=(), writes=(), dma_slot=None, n_dma=1):
        writes = list(writes) + [r for r in reads if r.startswith("ps")]
        reads = [r for r in reads if not r.startswith("ps")]
        deps = {}

        def add(tok):
            if tok is None:
                return
            k, v = tok
            if deps.get(k, 0) < v:
                deps[k] = v
        for r in reads:
            add(self.last_w.get(r))
        for w in writes:
            add(self.last_w.get(w))
            for t in self.readers.get(w, ()):
                add(t)
        waits = []
        for k, v in deps.items():
            if k == ("eng", "pe") and eng == "pe":
                continue
            if self.waited[eng].get(k, 0) >= v:
                continue
            self.waited[eng][k] = v
            waits.append((k, v))
        if dma_slot is not None:
            k = self._dma_sem(dma_slot)
            self.dma_count[dma_slot] += 16 * n_dma
            tok = (k, self.dma_count[dma_slot])
            inc = 16
        else:
            self.count[eng] += 1
            tok = (("eng", eng), self.count[eng])
            inc = 1
        self.streams[eng].append((waits, fn, tok[0], inc))
        for r in reads:
            self.readers.setdefault(r, []).append(tok)
        for w in writes:
            self.last_w[w] = tok
            self.readers[w] = []
        return tok

    def final_wait(self, eng, toks):
        self.streams[eng].append(([(k, v) for (k, v) in toks], None, None, 0))

    def emit(self):
        nc = self.nc
        with nc.Block() as block:
            def run(ename):
                def body(e):
                    for waits, fn, semk, inc in self.streams[ename]:
                        for k, v in waits:
                            e.wait_ge(self.sems[k], v)
                        if fn is None:
                            continue
                        res = fn(e)
                        if isinstance(res, (list, tuple)):
                            for r in res:
                                r.then_inc(self.sems[semk], inc)
                        else:
                            res.then_inc(self.sems[semk], inc)
                return body
            block.tensor(run("pe"))
            block.scalar(run("act"))
            block.vector(run("dve"))
            block.gpsimd(run("pool"))
            block.sync(run("sp"))


class _Done(Exception):
    pass


def build_nc(stage=None):
    nc = bass.Bass("TRN2", target_bir_lowering=False)
    dumps = []

    def din(name, shape, dt=F32):
        return nc.dram_tensor(name, list(shape), dt, kind="ExternalInput").ap()
    xk = din("xk", [NK, D])
    win_m = din("win_m", [24, 128, 16 * 128])
    win_k = din("win_k", [128, 16 * 320])
    win_w = din("win_w", [128, 16 * 16])
    wukT = din("wukT", [128, 8 * 256])
    wuv = din("wuv", [128, 2 * 8 * 128])
    wpool = din("wpool", [128, 4 * 2 * 256])
    wo = din("wo", [16, 128, 16 * 128])
    wup = din("wup", [88, 128, 16 * 128])
    wdn = din("wdn", [16, 128, 44 * 128])
    convp = din("convp", [128, 88 * 4])
    pscale_d = din("pscale", [128, 8])
    gk_d = din("gk", [1, 256])
    ln_d = din("lnp", [4, D])
    idn_d = din("idn", [128, 128])
    ones_d = din("ones", [128, 128])
    bd_d = din("bd", [NT, 8 * NT])
    bp_d = din("bp", [NT, 8 * NT])
    b31_d = din("b31", [NT, 8 * NT])
    tri01_d = din("tri01", [NT, NT])
    trineg_d = din("trineg", [NT, NT])
    kflag_d = din("kflag", [1, KOFF], U8)
    fix_d = din("fixc", [128, 1])
    ln1c_d = din("ln1c", [128, 32])
    out = nc.dram_tensor("out", [1024, D], F32, kind="ExternalOutput").ap()

    with ExitStack() as es:
      try:
          P = Prog(nc, es)

          def sb(name, shape, dt=F32):
              return es.enter_context(nc.sbuf_tensor("s_" + name, list(shape), dt))

          def psum(name):
              return es.enter_context(nc.psum_tensor("p_" + name, [128, 512], F32))

          def fr(ap):
              return ap.bitcast(F32)

          def stop(sid, items):
              if stage != sid:
                  return
              toks = []
              for name, ap, res in items:
                  shp = list(ap.shape)
                  d = nc.dram_tensor("dbg_" + name, shp, F32, kind="ExternalOutput").ap()
                  dumps.append("dbg_" + name)
                  toks.append(P.op("sp", lambda e, d=d, ap=ap: e.dma_start(out=d, in_=ap), reads=[res], dma_slot="dbg_" + name))
              P.final_wait("sp", toks)
              P.emit()
              raise _Done()

          ckvT = sb("ckvT", [128, 2, NK], F32R)
          ckv = sb("ckv", [128, KC, 256], F32R)
          kT2 = sb("kT2", [128, NK], F32R)
          wsm = sb("wsm", [128, 2048], F32R)
          wwi = sb("wwi", [128, 16, 16], F32R)
          ones = sb("ones", [128, 128], F32R)
          cat = sb("cat", [128, 16, G], F32R)
          wb = sb("wb", [128, 2, 2048], F32R)
          REG = sb("REG", [128, NFF * G], F32R)
          dT = sb("dT", [128, 2, G], F32R)
          qtmp = sb("qtmp", [128, 2, G], F32R)
          em = sb("em", [128, 2, 456], F32R)
          rl = sb("rl", [128, 2, 456], F32R)
          otmp = sb("otmp", [128, 2, 456])
          idnR = sb("idnR", [128, 128], F32R)
          idnB = sb("idnB", [128, 128], BF16)
          idn = sb("idn", [128, 128])
          Bd = sb("Bd", [128, 8, NT], F32R)
          Bp = sb("Bp", [128, 8, NT], F32R)
          kflag = sb("kflag", [128, KOFF], U8)
          trineg = sb("trineg", [128, NT])
          pw = sb("pw", [128, NIT + 2])
          convw = sb("convw", [128, 88, 4])
          pscale = sb("pscale", [128, 8])
          gkb = sb("gkb", [128, 256])
          halo = sb("halo", [128, 88, 2])
          fixc = sb("fixc", [128, 1])
          hmask = sb("hmask", [128, 2])
          junk = sb("junk", [128, NK], U8)
          widx = sb("widx", [128, TPG, 16])
          st = sb("st", [128, 64])
          st2 = sb("st2", [128, 64])
          ln1c = sb("ln1c", [128, 32])
          wcols = sb("wcols", [128, NIT + 2])
          SCR = sb("SCR", [128, 6144])
          PS = [psum("ps%d" % i) for i in range(8)]

          xTe = REG[:, 0:16 * GX].rearrange("p (k t) -> p k t", k=16)
          olat = REG[:, 0:16 * G].rearrange("p (c h t) -> p c h t", c=2, h=8)
          mixT = REG[:, 0:16 * G].rearrange("p (k t) -> p k t", k=16)
          o1 = 16 * GX
          qiT = REG[:, o1:o1 + 8 * G].rearrange("p (k t) -> p k t", k=8)
          o2 = o1 + 8 * G
          uT = REG[:, o2:o2 + 8 * GX].rearrange("p (k t) -> p k t", k=8)
          gT = REG[:, :].rearrange("p (k t) -> p k t", k=NFF)
          xT1 = REG[:, 0:2 * 16 * NT].rearrange("p (b k t) -> p b k t", b=2, k=16)
          wk = REG[:, 4096:4096 + 16 * 320].rearrange("p (k n) -> p k n", k=16)
          qabs = cat[:, :, :].rearrange("p (c h) t -> p c h t", c=2)
          scores = SCR[:, 0:NK]
          scb = [SCR[:, 0:NK], SCR[:, 3712:3712 + NK]]
          nmT = SCR[:, NK:NK + NK // 2].bitcast(BF16).rearrange("p (k t) -> p k t", k=KC)
          rd = SCR[:, NK + NK // 2:NK + NK // 2 + 456]
          xt = SCR[:, 0:D]
          lng = SCR[:, D:2 * D]
          lnb = SCR[:, 2 * D:3 * D]
          tA = SCR[:, 0:2 * GX].rearrange("p (c t) -> p c t", c=2)
          tB = SCR[:, 2 * GX:4 * GX].rearrange("p (c t) -> p c t", c=2)
          zc = SCR[:, 2048:2048 + 4 * G].rearrange("p (b t) -> p b t", b=4)
          gtmp = SCR[:, 4096:4096 + 2 * G].rearrange("p (b t) -> p b t", b=2)

          cnt = st[:, 0:1]
          mid = st[:, 1:2]
          tstep = st[:, 2:3]
          rmin = st[:, 3:4]
          dwid = st[:, 4:5]
          m8 = st[:, 8:16]
          ssq = st[:, 16:17]
          rstd = st[:, 17:18]
          bnst = st[:, 24:48].rearrange("p (a b) -> p a b", a=4)
          mv = st[:, 48:50]
          sd = st[:, 50:51]

          def ld(eng, dst, src, res, slot):
              return P.op(eng, lambda e: e.dma_start(out=dst, in_=src), writes=[res], dma_slot=slot)
          ld("sp", idn[:, :], idn_d[:, :], "idn", "c0")
          ld("pool", ones[:, :], ones_d[:, :], "ones", "c1")
          ld("sp", SCR[0:NT, 1024:1024 + 8 * NT], bd_d[:, :], "scr_bd", "c2")
          ld("sp", SCR[0:NT, 2048:2048 + 8 * NT], bp_d[:, :], "scr_bp", "c3")
          ld("sp", SCR[0:NT, 0:8 * NT], b31_d[:, :], "scr_b31", "c4")
          ld("pool", idnR[:, :], idn_d[:, :], "idnR", "c5")
          ld("sp", trineg[0:NT, :], trineg_d[:, :], "trineg", "c6")
          ld("sp", kflag[0:NT, :], kflag_d[0:1, :].partition_broadcast(NT), "kflag", "c7")
          ld("sp", convw[:, :, :].rearrange("p a b -> p (a b)"), convp[:, :], "convw", "c8")
          ld("sp", pscale[:, :], pscale_d[:, :], "pscale", "c9")
          ld("sp", gkb[0:NT, :], gk_d[0:1, :].partition_broadcast(NT), "gkb", "c10")
          ld("sp", fixc[:, :], fix_d[:, :], "fixc", "c11")
          ld("sp", ln1c[:, :], ln1c_d[:, :], "ln1c", "c14")
          ld("pool", wwi[:, :, :].rearrange("p a b -> p (a b)"), win_w[:, :], "wwi", "c12")
          ld("pool", wk.rearrange("p a b -> p (a b)") if False else REG[:, 4096:4096 + 16 * 320], win_k[:, :], "wk", "c13")
          P.op("pool", lambda e: e.memset(hmask[:, :], 0.0), writes=["hmask"])
          P.op("pool", lambda e: e.memset(hmask[0:64, 0:1], 1.0), writes=["hmask"])
          P.op("pool", lambda e: e.memset(hmask[64:128, 1:2], 1.0), writes=["hmask"])
          for n in range(NIT + 2):
              P.op("pool", lambda e, n=n: e.memset(pw[:, n:n + 1], 2.0 ** (-n)), writes=["pw"])
          P.op("dve", lambda e: e.tensor_copy(out=idnB[:, :], in_=idn[:, :]), reads=["idn"], writes=["idnB"])
          for (c0, Bt, nm, sn) in ((1024, Bd, "Bd", "scr_bd"), (2048, Bp, "Bp", "scr_bp")):
              P.op("dve", lambda e, c0=c0: e.tensor_tensor(out=SCR[0:NT, c0:c0 + 8 * NT], in0=SCR[0:NT, c0:c0 + 8 * NT],
                                                           in1=SCR[0:NT, 0:8 * NT], op=ALU.subtract),
                   reads=["scr_b31"], writes=[sn])
              P.op("dve", lambda e, c0=c0, Bt=Bt: e.tensor_scalar(out=Bt[0:NT].rearrange("p h t -> p (h t)"),
                                                                  in0=SCR[0:NT, c0:c0 + 8 * NT], scalar1=128.0 ** 0.5,
                                                                  scalar2=None, op0=ALU.mult),
                   reads=[sn], writes=[nm])
          P.retire(["scr_b31", "scr_bd", "scr_bp"], ["xt"])
          stop(1, [("Bd", fr(Bd[0:NT]), "Bd"), ("Bp", fr(Bp[0:NT]), "Bp")])
          def transposes_to(ps_list, srcs, rows, cols, reads, tag):
              toks = []
              nb = (len(srcs) + 3) // 4
              for b in range(nb):
                  sub = srcs[4 * b:4 * b + 4]

                  def f(e, sub=sub, b=b):
                      last = None
                      for i, s in enumerate(sub):
                          last = e.transpose(ps_list[b][0:cols, i * rows:(i + 1) * rows], s, idn[0:rows, 0:rows])
                      return last
                  P.op("pe", f, reads=list(reads) + ["idn"], writes=[tag + str(b)])
              return nb

          for kc in range(KC):
              xb = kc % 2
              xtk = xt if xb == 0 else SCR[:, 2048:4096]
              xtn = "xt" if xb == 0 else "xtB"
              P.op("sp", lambda e, kc=kc, xtk=xtk: e.dma_start(out=xtk[0:NT, :], in_=xk[kc * NT:(kc + 1) * NT, :]),
                   writes=[xtn], dma_slot=xtn)
              srcs = [xtk[0:NT, m * 128:(m + 1) * 128] for m in range(16)]
              for b in range(4):
                  def f(e, b=b, srcs=srcs):
                      last = None
                      for i in range(4):
                          last = e.transpose(PS[b][:, i * NT:(i + 1) * NT], srcs[4 * b + i], idn[0:NT, 0:NT])
                      return last
                  P.op("pe", f, reads=[xtn, "idn"], writes=["ps%d" % b])
                  P.op("act" if b % 2 == 0 else "dve",
                       (lambda e, b=b, xb=xb: e.activation(out=xT1[:, xb, 4 * b:4 * b + 4, :],
                                                           in_=PS[b][:, 0:4 * NT].rearrange("p (k t) -> p k t", k=4),
                                                           func=AF.Copy)) if b % 2 == 0 else
                       (lambda e, b=b, xb=xb: e.tensor_copy(out=xT1[:, xb, 4 * b:4 * b + 4, :],
                                                            in_=PS[b][:, 0:4 * NT].rearrange("p (k t) -> p k t", k=4))),
                       reads=["ps%d" % b], writes=["xT1_%d_%d" % (xb, b)])

              pm = 4 + 2 * (kc % 2)
              pmn = "ps%d" % pm
              ptn = "ps%d" % (pm + 1)
              kd = SCR[0:NT, 4096 + 128 * (kc % 2):4096 + 128 * (kc % 2) + 128]
              kdn = "kd%d" % (kc % 2)

              def fmm(e, xb=xb, pm=pm):
                  last = None
                  for k in range(16):
                      last = e.matmul(PS[pm][0:NT, 0:320], xT1[:, xb, k, :], wk[:, k, :], start=(k == 0), stop=(k == 15))
                  return last
              P.op("pe", fmm, reads=["xT1_%d_%d" % (xb, b_) for b_ in range(4)] + ["wk"], writes=[pmn])
              P.op("act", lambda e, pm=pm: e.activation(out=junk[0:NT, 0:256], in_=PS[pm][0:NT, 0:256], func=AF.Square,
                                                        accum_out=ssq[0:NT]),
                   reads=[pmn], writes=["ssq", "junk"])
              P.op("act", lambda e: e.activation(out=sd[0:NT], in_=ssq[0:NT], func=AF.Sqrt, scale=1.0 / 256.0, bias=EPS),
                   reads=["ssq"], writes=["sd"])
              P.op("dve", lambda e: e.reciprocal(out=rstd[0:NT], in_=sd[0:NT]), reads=["sd"], writes=["rstd"])
              P.op("dve", lambda e, kc=kc, pm=pm: e.scalar_tensor_tensor(out=ckv[0:NT, kc, :], in0=PS[pm][0:NT, 0:256],
                                                                         scalar=rstd[0:NT], in1=gkb[0:NT, :],
                                                                         op0=ALU.mult, op1=ALU.mult),
                   reads=[pmn, "rstd", "gkb"], writes=["ckv%d" % kc])
              P.op("act", lambda e, pm=pm, kd=kd: e.activation(out=kd.rearrange("p (a b) -> p a b", a=2),
                                                               in_=PS[pm][0:NT, 256:320].unsqueeze(1).to_broadcast([NT, 2, 64]),
                                                               func=AF.Copy),
                   reads=[pmn], writes=[kdn])

              def small_tr(kc=kc, pm=pm, kd=kd, kdn=kdn, ptn=ptn):
                  def ftr(e):
                      e.transpose(PS[pm + 1][:, 0:NT], fr(ckv[0:NT, kc, 0:128]), idn[0:NT, 0:NT])
                      e.transpose(PS[pm + 1][:, NT:2 * NT], fr(ckv[0:NT, kc, 128:256]), idn[0:NT, 0:NT])
                      return e.transpose(PS[pm + 1][:, 2 * NT:3 * NT], kd, idn[0:NT, 0:NT])
                  P.op("pe", ftr, reads=["ckv%d" % kc, kdn, "idn"], writes=[ptn])
                  P.op("act", lambda e: e.activation(out=ckvT[:, :, kc * NT:(kc + 1) * NT],
                                                     in_=PS[pm + 1][:, 0:2 * NT].rearrange("p (c t) -> p c t", c=2),
                                                     func=AF.Copy),
                       reads=[ptn], writes=["ckvT%d" % kc])
                  P.op("dve", lambda e: e.tensor_copy(out=kT2[:, kc * NT:(kc + 1) * NT], in_=PS[pm + 1][:, 2 * NT:3 * NT]),
                       reads=[ptn], writes=["kT2_%d" % kc])
              if kc > 0:
                  deferred_tr()
              deferred_tr = small_tr
          deferred_tr()
          CKV_ALL = ["ckv%d" % k for k in range(KC)]
          CKVT_ALL = ["ckvT%d" % k for k in range(KC)]
          KT2_ALL = ["kT2_%d" % k for k in range(KC)]

          stop(2, [("ckv", fr(ckv[0:NT]), r) for r in CKV_ALL[-1:]] + [("kT2", fr(kT2[:, :]), KT2_ALL[-1]), ("ckvT", fr(ckvT[:, :, :]), CKVT_ALL[-1])])
          out_toks = []
          wslot = [0]

          def stream_w(src_ap, nelem=2048, three=False):
              ns = 3 if three else 2
              s = wslot[0] % ns
              wslot[0] += 1
              res = "wb%d" % s if s < 2 else "wsm"
              dst = wb[:, s, :] if s < 2 else wsm[:, :]
              P.op("pool", lambda e: e.dma_start(out=dst[:, 0:nelem], in_=src_ap), writes=[res], dma_slot=res)
              return dst, res

          for g in range(NGR):
              ck0 = KOFF + g * G
              P.retire(["gT%d" % i for i in range(NFF)] + ["xT1_%d_%d" % (a_, b_) for a_ in range(2) for b_ in range(4)] + ["wk", "mixT"], ["xTe0", "xTe1", "xTe2", "xTe3", "qiT", "uT"])
              P.retire(["scores0", "scores1", "nmT", "rd", "lng", "lnb", "zc0", "zc1", "zc2", "zc3", "gtmp0", "gtmp1", "tA", "tB", "kd0", "kd1", "xtB"], ["xt"])
              pieces = [(ck0 - HALO, HALO, 0)] + [(ck0 + j * NT, NT, HALO + j * NT) for j in range(TPG)]
              P.retire(["xt"], ["xtB"])
              for pi, (r0, nr, c0) in enumerate(pieces):
                  xta = xt if pi % 2 == 0 else SCR[:, 2048:4096]
                  xtn = "xt" if pi % 2 == 0 else "xtB"
                  P.op("sp", lambda e, r0=r0, nr=nr, xta=xta: e.dma_start(out=xta[0:nr, :], in_=xk[r0:r0 + nr, :]),
                       writes=[xtn], dma_slot=xtn)
                  for b in range(4):
                      def f(e, b=b, nr=nr, xta=xta):
                          last = None
                          for i in range(4):
                              m = 4 * b + i
                              last = e.transpose(PS[b][:, i * nr:(i + 1) * nr], xta[0:nr, m * 128:(m + 1) * 128],
                                                 idn[0:nr, 0:nr])
                          return last
                      P.op("pe", f, reads=[xtn, "idn"], writes=["ps%d" % b])
                      if b % 2 == 0:
                          P.op("act", lambda e, b=b, nr=nr, c0=c0: e.activation(
                              out=xTe[:, 4 * b:4 * b + 4, c0:c0 + nr],
                              in_=PS[b][:, 0:4 * nr].rearrange("p (k t) -> p k t", k=4), func=AF.Copy),
                              reads=["ps%d" % b], writes=["xTe%d" % b])
                      else:
                          P.op("dve", lambda e, b=b, nr=nr, c0=c0: e.tensor_copy(
                              out=xTe[:, 4 * b:4 * b + 4, c0:c0 + nr],
                              in_=PS[b][:, 0:4 * nr].rearrange("p (k t) -> p k t", k=4)),
                              reads=["ps%d" % b], writes=["xTe%d" % b])
              wukS = REG[:, 11328:11328 + 2048]
              P.retire(["gT%d" % i for i in range(NFF)] + ["wk", "diagW", "qz"], ["wukS"])
              P.op("pool", lambda e: e.dma_start(out=wukS, in_=wukT[:, :]), writes=["wukS"], dma_slot="wukS")
              for j in range(TPG):
                  def fw(e, j=j):
                      last = None
                      for k in range(16):
                          last = e.matmul(PS[6][0:NT, 0:16], xTe[:, k, HALO + j * NT:HALO + (j + 1) * NT], wwi[:, k, :],
                                          start=(k == 0), stop=(k == 15))
                      return last
                  P.op("pe", fw, reads=["xTe0", "xTe1", "xTe2", "xTe3"] + ["wwi"], writes=["ps6"])
                  P.op("dve", lambda e, j=j: e.tensor_copy(out=widx[0:NT, j, :], in_=PS[6][0:NT, 0:16]),
                       reads=["ps6"], writes=["widx"])
              P.retire(["cat"], ["qabs"])
              for ci in range(24):
                  s, res = stream_w(win_m[ci, :, :], three=True)
                  pb = 4 + (ci % 2)
                  isu = 8 <= ci < 16
                  n0 = 0 if isu else HALO

                  def fm(e, s=s, pb=pb, n0=n0):
                      last = None
                      for k in range(16):
                          last = e.matmul(PS[pb][:, 0:GX - n0], s[:, k * 128:(k + 1) * 128], xTe[:, k, n0:GX],
                                          start=(k == 0), stop=(k == 15))
                      return last
                  P.op("pe", fm, reads=[res] + ["xTe0", "xTe1", "xTe2", "xTe3"], writes=["ps%d" % pb])
                  if ci < 8:
                      P.op("act", lambda e, ci=ci, pb=pb: e.activation(out=qiT[:, ci, :], in_=PS[pb][:, 0:G], func=AF.Copy),
                           reads=["ps%d" % pb], writes=["qiT"])
                  elif isu:
                      P.op("act", lambda e, ci=ci, pb=pb: e.activation(out=uT[:, ci - 8, :], in_=PS[pb][:, 0:GX], func=AF.Copy),
                           reads=["ps%d" % pb], writes=["uT"])
                  else:
                      h = ci - 16
                      qb = h % 2
                      P.op("act", lambda e, pb=pb, qb=qb: e.activation(out=qtmp[:, qb, :], in_=PS[pb][:, 0:G], func=AF.Copy),
                           reads=["ps%d" % pb], writes=["qtmp%d" % qb])
                      for cc in range(2):
                          P.op("pe", lambda e, h=h, cc=cc, qb=qb: e.matmul(
                              PS[6 + cc][:, 0:G], wukS[:, h * 256 + cc * 128:h * 256 + (cc + 1) * 128], qtmp[:, qb, :],
                              start=True, stop=True),
                              reads=["wukS", "qtmp%d" % qb], writes=["ps%d" % (6 + cc)])
                          P.op("dve", lambda e, h=h, cc=cc: e.tensor_copy(out=qabs[:, cc, h, :], in_=PS[6 + cc][:, 0:G]),
                               reads=["ps%d" % (6 + cc)], writes=["qabs"])
              if g == 0:
                  stop(3, [("qabs", fr(qabs), "qabs"), ("qiT", fr(qiT), "qiT"), ("uT", fr(uT), "uT"), ("widx", widx[0:NT], "widx")])
              wo_pre = [stream_w(wo[m, :, :], three=False) for m in range(2)]
              P.retire(["xt", "xtB", "lng", "lnb"], ["scores0", "scores1", "nmT", "rd"])
              P.retire(["xTe0", "xTe1", "xTe2", "xTe3"], ["olat"])
              P.retire(["gT%d" % i for i in range(NFF)] + ["wk", "wukS"], ["diagW", "qz"])

              def front_ops(j):
                  J = g * TPG + j
                  scores = scb[j % 2]
                  SC = "scores%d" % (j % 2)
                  nch = 11 + J
                  nk = nch * NT
                  qc0 = j * NT
                  dcol = (nch - 1) * NT
                  nbk = (nk + 455) // 456
                  itc = [0]

                  diagW = REG[0:NT, 11328:11328 + 16 * NT].rearrange("p (h q) -> p h q", h=16)
                  qz = REG[:, 11328 + 16 * NT:11328 + 32 * NT].rearrange("p (c t q) -> p c t q", c=8, t=2)
                  seq = [(bk, h) for bk in range(nbk) for h in range(16)]

                  def dots(i):
                      bk, h = seq[i]
                      hc, hp = h // 2, (h % 2) * 64
                      c0 = bk * 456
                      w = min(456, nk - c0)
                      pb = i % 2
                      P.op("pe", lambda e: e.matmul(
                          PS[5 + pb][0:NT, 0:w], qz[:, hc, h % 2, :], kT2[:, c0:c0 + w], start=True, stop=True),
                          reads=["qz"] + KT2_ALL[c0 // NT:(c0 + w) // NT], writes=["ps%d" % (5 + pb)])
                      P.op("act", lambda e: e.activation(out=rl[0:NT, pb, 0:w], in_=PS[5 + pb][0:NT, 0:w], func=AF.Relu),
                           reads=["ps%d" % (5 + pb)], writes=["rl%d" % pb])

                  def accum(i):
                      bk, h = seq[i]
                      c0 = bk * 456
                      w = min(456, nk - c0)
                      pb = i % 2
                      P.op("pe", lambda e: e.matmul(PS[7][0:NT, 0:w], diagW[:, h, :], rl[0:NT, pb, 0:w],
                                                    start=(h == 0), stop=(h == 15)),
                           reads=["rl%d" % pb, "diagW"], writes=["ps7"])
                      if h == 15:
                          P.op("act", lambda e: e.activation(out=scores[0:NT, c0:c0 + w], in_=PS[7][0:NT, 0:w], func=AF.Copy),
                               reads=["ps7"], writes=[SC])

                  def item(i):
                      if i == 0:
                          P.op("dve", lambda e: e.tensor_tensor(
                              out=diagW, in0=idn[0:NT, 0:NT].unsqueeze(1).to_broadcast([NT, 16, NT]),
                              in1=widx[0:NT, j, :].unsqueeze(2).to_broadcast([NT, 16, NT]), op=ALU.mult),
                              reads=["idn", "widx"], writes=["diagW"])
                          P.op("dve", lambda e: e.tensor_tensor(
                              out=qz, in0=fr(qiT[:, :, qc0:qc0 + NT]).unsqueeze(2).to_broadcast([128, 8, 2, NT]),
                              in1=hmask[:, :].unsqueeze(1).unsqueeze(3).to_broadcast([128, 8, 2, NT]), op=ALU.mult),
                              reads=["qiT", "hmask"], writes=["qz"])
                          dots(0)
                      if i + 1 < len(seq):
                          dots(i + 1)
                      accum(i)
                  items = [(lambda i=i: item(i)) for i in range(len(seq))]

                  Bl, Tl = [], []

                  def build_tail():
                      OP = lambda *a, **k: Bl.append((a, k))
                      OT = lambda *a, **k: Tl.append((a, k))
                      OP("dve", lambda e: e.max(out=m8[0:NT], in_=scores[0:NT, 0:nk]), reads=[SC], writes=["m8"])
                      OP("dve", lambda e: e.tensor_reduce(out=rmin[0:NT], in_=scores[0:NT, 0:nk], op=ALU.min, axis=AX.X),
                           reads=[SC], writes=["rmin"])
                      OP("dve", lambda e: e.tensor_tensor(out=dwid[0:NT], in0=m8[0:NT, 0:1], in1=rmin[0:NT], op=ALU.subtract),
                           reads=["m8", "rmin"], writes=["dwid"])
                      OP("dve", lambda e: e.tensor_scalar(out=dwid[0:NT], in0=dwid[0:NT], scalar1=1.002, scalar2=2e-6,
                                                            op0=ALU.mult, op1=ALU.add), reads=["dwid"], writes=["dwid"])
                      OP("dve", lambda e: e.tensor_scalar(out=wcols[0:NT, :], in0=pw[0:NT, :], scalar1=dwid[0:NT], scalar2=None,
                                                            op0=ALU.mult), reads=["pw", "dwid"], writes=["wcols"])
                      OP("dve", lambda e: e.scalar_tensor_tensor(out=mid[0:NT], in0=dwid[0:NT], scalar=-0.001, in1=rmin[0:NT],
                                                                   op0=ALU.mult, op1=ALU.add),
                           reads=["rmin", "dwid"], writes=["mid"])
                      OP("dve", lambda e: e.tensor_tensor(out=mid[0:NT], in0=mid[0:NT], in1=wcols[0:NT, 1:2], op=ALU.add),
                           reads=["wcols"], writes=["mid"])
                      OP("dve", lambda e: e.scalar_tensor_tensor(out=scores[0:NT, 0:KOFF], in0=kflag[0:NT, :], scalar=NEG,
                                                                   in1=scores[0:NT, 0:KOFF], op0=ALU.mult, op1=ALU.add),
                           reads=["kflag", "rmin", "m8"], writes=[SC])
                      OP("dve", lambda e: e.tensor_tensor(out=scores[0:NT, dcol:dcol + NT], in0=scores[0:NT, dcol:dcol + NT],
                                                            in1=trineg[0:NT, :], op=ALU.add),
                           reads=["trineg"], writes=[SC])
                      for n in range(1, NIT + 1):
                          OP("dve", lambda e: e.tensor_scalar(out=junk[0:NT, 0:nk], in0=scores[0:NT, 0:nk], scalar1=mid[0:NT],
                                                                scalar2=None, op0=ALU.is_ge, op1=ALU.add, accum_out=cnt[0:NT]),
                               reads=[SC, "mid"], writes=["cnt", "junk"])
                          OP("dve", lambda e, n=n: e.tensor_scalar(out=tstep[0:NT], in0=cnt[0:NT], scalar1=255.5,
                                                                     scalar2=wcols[0:NT, n:n + 1], op0=ALU.is_ge, op1=ALU.mult),
                               reads=["cnt", "wcols"], writes=["tstep"])
                          nn = n + 1 if n < NIT else n
                          OP("dve", lambda e, nn=nn: e.scalar_tensor_tensor(out=mid[0:NT], in0=tstep[0:NT], scalar=wcols[0:NT, nn:nn + 1],
                                                                              in1=mid[0:NT], op0=ALU.subtract, op1=ALU.add),
                               reads=["tstep", "wcols"], writes=["mid"])
                      OP("dve", lambda e: e.tensor_scalar(out=scores[0:NT, 0:nk], in0=scores[0:NT, 0:nk], scalar1=mid[0:NT],
                                                            scalar2=None, op0=ALU.is_ge),
                           reads=["mid"], writes=[SC])
                      nb4 = (nch + 3) // 4
                      for b in range(nb4):
                          k0 = 4 * b
                          k1 = min(nch, k0 + 4)

                          def ft(e, k0=k0, k1=k1, b=b):
                              last = None
                              for i, kc in enumerate(range(k0, k1)):
                                  last = e.transpose(PS[5 + (b % 2)][0:NT, i * NT:(i + 1) * NT], scores[0:NT, kc * NT:(kc + 1) * NT],
                                                     idn[0:NT, 0:NT])
                              return last
                          OT("pe", ft, reads=[SC, "idn"], writes=["ps%d" % (5 + b % 2)])
                          OT("act", lambda e, k0=k0, k1=k1, b=b: e.activation(
                              out=nmT[0:NT, k0:k1, :], in_=PS[5 + (b % 2)][0:NT, 0:(k1 - k0) * NT].rearrange("p (k t) -> p k t", k=k1 - k0),
                              func=AF.Identity, scale=MBIG, bias=-MBIG), reads=["ps%d" % (5 + b % 2)], writes=["nmT"])
                  build_tail()
                  return items, Bl, Tl

              def back_ops(j):
                  J = g * TPG + j
                  nch = 11 + J
                  qc0 = j * NT
                  its = []
                  cnt_it = [0]

                  def chunk_l(hh, kc, pb):
                      def fl(e):
                          near = kc >= nch - 2
                          e.matmul(PS[pb][0:NT, 0:456], ckvT[:, 0, kc * NT:(kc + 1) * NT],
                                   qabs[:, 0, 4 * hh:4 * hh + 4, qc0:qc0 + NT], start=True, stop=False)
                          e.matmul(PS[pb][0:NT, 0:456], ckvT[:, 1, kc * NT:(kc + 1) * NT],
                                   qabs[:, 1, 4 * hh:4 * hh + 4, qc0:qc0 + NT], start=False, stop=False)
                          if near:
                              Bt = Bd if kc == nch - 1 else Bp
                              e.matmul(PS[pb][0:NT, 0:456], idnR[0:NT, 0:NT], Bt[0:NT, 4 * hh:4 * hh + 4, :], start=False, stop=False)
                          return e.matmul(PS[pb][0:NT, 0:456], idnB[0:NT, 0:NT],
                                          nmT[0:NT, kc, :].unsqueeze(1).to_broadcast([NT, 4, NT]), start=False, stop=True)
                      P.op("pe", fl, reads=["ckvT%d" % kc, "qabs", "nmT", "Bd", "Bp", "idnR", "idnB"], writes=["ps%d" % pb])
                      P.op("act", lambda e: e.activation(out=em[0:NT, pb, :], in_=PS[pb][0:NT, 0:456], func=AF.Exp,
                                                         scale=128.0 ** -0.5),
                           reads=["ps%d" % pb], writes=["em%d" % pb])

                  def chunk_pv(hh, kc, pb):
                      def fpv(e):
                          st_, sp_ = (kc == 0), (kc == nch - 1)
                          e.matmul(PS[2][:, 0:456], ckv[0:NT, kc, 0:128], em[0:NT, pb, :], start=st_, stop=sp_)
                          e.matmul(PS[3][:, 0:456], ckv[0:NT, kc, 128:256], em[0:NT, pb, :], start=st_, stop=sp_)
                          return e.matmul(PS[4][:, 0:456], ones[0:NT, :], em[0:NT, pb, :], start=st_, stop=sp_)
                      P.op("pe", fpv, reads=["ckv%d" % kc, "em%d" % pb, "ones"], writes=["ps2", "ps3", "ps4"])
                      if kc == nch - 1:
                          P.op("act", lambda e: e.activation(out=rd[:, :], in_=PS[4][:, 0:456], func=AF.Copy),
                               reads=["ps4"], writes=["rd"])
                          for cc in range(2):
                              P.op("act", lambda e, cc=cc: e.activation(out=otmp[:, cc, :], in_=PS[2 + cc][:, 0:456], func=AF.Copy),
                                   reads=["ps%d" % (2 + cc)], writes=["otmp%d" % cc])
                          P.op("dve", lambda e: e.reciprocal(out=rd[:, :], in_=rd[:, :]), writes=["rd"])
                          for cc in range(2):
                              P.op("dve", lambda e, cc=cc: e.tensor_tensor(
                                  out=olat[:, cc, 4 * hh:4 * hh + 4, qc0:qc0 + NT],
                                  in0=otmp[:, cc, :].rearrange("p (h t) -> p h t", h=4),
                                  in1=rd[:, :].rearrange("p (h t) -> p h t", h=4), op=ALU.mult),
                                  reads=["otmp%d" % cc, "rd"], writes=["olat"])
                  seq = [(hh, kc, i % 2) for i, (hh, kc) in enumerate((hh, kc) for hh in range(2) for kc in range(nch))]
                  for i, (hh, kc, pb) in enumerate(seq):
                      def it(i=i):
                          if i == 0:
                              chunk_l(*seq[0])
                          if i + 1 < len(seq):
                              chunk_l(*seq[i + 1])
                          chunk_pv(*seq[i])
                      its.append(it)
                  return its

              fr_ = [front_ops(j) for j in range(TPG)]
              bk_ = [None] * TPG

              def emit_all(ops):
                  for a, k in ops:
                      P.op(*a, **k)
              for t in range(TPG + 2):
                  Il = fr_[t][0] if t < TPG else []
                  Bl = fr_[t - 1][1] if 0 <= t - 1 < TPG else []
                  Tl = fr_[t - 1][2] if 0 <= t - 1 < TPG else []
                  if 0 <= t - 2 < TPG:
                      bk_[t - 2] = back_ops(t - 2)
                      Kl = bk_[t - 2]
                  else:
                      Kl = []
                  lead = Il if Il else (Kl if Kl else [None])
                  n = len(lead)
                  dk = db = 0
                  for i in range(n):
                      if Il:
                          Il[i]()
                      wk_ = (len(Kl) * (i + 1)) // n
                      while dk < wk_:
                          Kl[dk]()
                          dk += 1
                      wb_ = (len(Bl) * (i + 1)) // n
                      while db < wb_:
                          a, k = Bl[db]
                          P.op(*a, **k)
                          db += 1
                  emit_all(Tl)
                  if g == 0 and t == 1:
                      stop(4, [("mask", scb[0][0:NT, 0:11 * NT], "scores0"), ("thr", mid[0:NT], "mid")])
              if g == 0:
                  stop(5, [("olat", fr(olat), "olat")])
              P.op("pool", lambda e: e.dma_start(out=wsm[:, :], in_=wuv[:, :]), writes=["wsm"], dma_slot="wsm")
              P.retire(["qabs"], ["cat"])
              for h in range(8):
                  pb = 6 + (h % 2)

                  def fa(e, h=h, pb=pb):
                      e.matmul(PS[pb][:, 0:G], wsm[:, (0 * 8 + h) * 128:(0 * 8 + h + 1) * 128], olat[:, 0, h, :], start=True, stop=False)
                      return e.matmul(PS[pb][:, 0:G], wsm[:, (1 * 8 + h) * 128:(1 * 8 + h + 1) * 128], olat[:, 1, h, :],
                                      start=False, stop=True)
                  P.op("pe", fa, reads=["wsm", "olat"], writes=["ps%d" % pb])
                  P.op("act", lambda e, h=h, pb=pb: e.activation(out=cat[:, h, :], in_=PS[pb][:, 0:G], func=AF.Copy),
                       reads=["ps%d" % pb], writes=["cat"])
              P.op("pool", lambda e: e.dma_start(out=wsm[:, :], in_=wpool[:, :]), writes=["wsm"], dma_slot="wsm")
              P.retire(["scores0", "scores1", "nmT", "rd"], ["tA", "tB"])
              for gi in range(4):
                  cur = fr(uT[:, 2 * gi:2 * gi + 2, :])
                  bufs = [tA, tB]
                  for lev in range(gi + 1):
                      sh = 1 << lev
                      nxt = bufs[lev % 2]
                      vs = sh - 1
                      P.op("dve", lambda e, cur=cur, nxt=nxt, sh=sh, vs=vs: e.tensor_tensor(
                          out=nxt[:, :, vs + sh:GX], in0=cur[:, :, vs + sh:GX], in1=cur[:, :, vs:GX - sh], op=ALU.add),
                          reads=["uT", "tA", "tB"], writes=["tA" if lev % 2 == 0 else "tB"])
                      cur = nxt
                  wn = "tA" if gi % 2 == 0 else "tB"
                  if gi == 3 and g == 0:
                      P.op("dve", lambda e, cur=cur: e.tensor_scalar(out=cur[:, :, HALO:HALO + 1], in0=cur[:, :, HALO:HALO + 1],
                                                                     scalar1=fixc[:, 0:1], scalar2=None, op0=ALU.mult),
                           reads=["fixc"], writes=[wn])
                  P.op("dve", lambda e, cur=cur, gi=gi: e.scalar_tensor_tensor(
                      out=dT[:, :, :], in0=cur[:, :, HALO:GX], scalar=1.0 / (2 << gi), in1=fr(uT[:, 2 * gi:2 * gi + 2, HALO:GX]),
                      op0=ALU.mult, op1=ALU.subtract), reads=[wn, "uT"], writes=["dT"])
                  for oc in range(2):
                      pb = 6 + oc

                      def fp(e, gi=gi, oc=oc, pb=pb):
                          b0 = (gi * 2 + 0) * 256 + oc * 128
                          b1 = (gi * 2 + 1) * 256 + oc * 128
                          e.matmul(PS[pb][:, 0:G], wsm[:, b0:b0 + 128], dT[:, 0, :], start=True, stop=False)
                          return e.matmul(PS[pb][:, 0:G], wsm[:, b1:b1 + 128], dT[:, 1, :], start=False, stop=True)
                      P.op("pe", fp, reads=["wsm", "dT"], writes=["ps%d" % pb])
                      P.op("act", lambda e, gi=gi, oc=oc, pb=pb: e.activation(
                          out=cat[:, 8 + gi * 2 + oc, :], in_=PS[pb][:, 0:G], func=AF.Identity,
                          scale=pscale[:, gi * 2 + oc:gi * 2 + oc + 1]),
                          reads=["ps%d" % pb, "pscale"], writes=["cat"])
              if g == 0:
                  stop(6, [("cat", fr(cat[:, :, :]), "cat")])
              P.retire(["olat"] + ["xTe0", "xTe1", "xTe2", "xTe3"], ["mixT"])
              for m in range(16):
                  s, res = wo_pre[m] if m < 2 else stream_w(wo[m, :, :], three=True)
                  pb = 4 + (m % 2)

                  def fo(e, s=s, pb=pb):
                      last = None
                      for k in range(16):
                          last = e.matmul(PS[pb][:, 0:G], s[:, k * 128:(k + 1) * 128], cat[:, k, :],
                                          start=(k == 0), stop=(k == 15))
                      return last
                  P.op("pe", fo, reads=[res, "cat"], writes=["ps%d" % pb])
                  P.op("act", lambda e, m=m, pb=pb: e.activation(out=mixT[:, m, :], in_=PS[pb][:, 0:G], func=AF.Copy),
                       reads=["ps%d" % pb], writes=["mixT"])
              if g == 0:
                  stop(7, [("mixT", fr(mixT), "mixT")])
              P.retire(["tA", "tB", "scores0", "scores1", "nmT", "rd"], ["xt", "lng", "lnb"])
              P.retire(["cat"], ["h1T"])

              def layer_norm_tile(src_views, resid_fn, W, wn, par, affine):
                  S = st if par == 0 else st2
                  bnst_, mv_, sd_, rstd_ = (S[:, 24:48].rearrange("p (a b) -> p a b", a=4), S[:, 48:50], S[:, 50:51], S[:, 17:18])
                  sn = "lnst%d" % par
                  for b in range(4):
                      def f(e, b=b):
                          last = None
                          for i in range(4):
                              last = e.transpose(PS[b][0:NT, i * 128:(i + 1) * 128], src_views[4 * b + i], idn[:, :])
                          return last
                      P.op("pe", f, reads=["mixT", "h1T", "idn"], writes=["ps%d" % b])
                      if resid_fn is not None:
                          resid_fn(b)
                  for b in range(4):
                      if resid_fn is not None:
                          P.op("dve", lambda e, b=b: e.bn_stats(out=bnst_[0:NT, b, :], in_=W[0:NT, b * 512:(b + 1) * 512]),
                               reads=[wn], writes=[sn])
                      else:
                          P.op("dve", lambda e, b=b: e.bn_stats(out=bnst_[0:NT, b, :], in_=PS[b][0:NT, :]),
                               reads=["ps%d" % b], writes=[sn])
                  P.op("dve", lambda e: e.bn_aggr(out=mv_[0:NT], in_=bnst_[0:NT].rearrange("p a b -> p (a b)")),
                       reads=[sn], writes=[sn])
                  P.op("act", lambda e: e.activation(out=sd_[0:NT], in_=mv_[0:NT, 1:2], func=AF.Sqrt, bias=EPS),
                       reads=[sn], writes=[sn])
                  P.op("dve", lambda e: e.reciprocal(out=rstd_[0:NT], in_=sd_[0:NT]), reads=[sn], writes=[sn])
                  nmr_ = S[:, 18:19]
                  if resid_fn is None:
                      P.op("dve", lambda e: e.scalar_tensor_tensor(out=nmr_[0:NT], in0=mv_[0:NT, 0:1], scalar=-1.0, in1=rstd_[0:NT],
                                                                   op0=ALU.mult, op1=ALU.mult), reads=[sn], writes=[sn])
                  if resid_fn is not None:
                      P.op("dve", lambda e: e.tensor_scalar(out=W[0:NT, :], in0=W[0:NT, :], scalar1=mv_[0:NT, 0:1], scalar2=rstd_[0:NT],
                                                            op0=ALU.subtract, op1=ALU.mult),
                           reads=[sn], writes=[wn])
                  else:
                      for b in range(4):
                          P.op("dve" if b % 2 == 0 else "act",
                               (lambda e, b=b: e.tensor_scalar(out=W[0:NT, b * 512:(b + 1) * 512], in0=PS[b][0:NT, :],
                                                               scalar1=mv_[0:NT, 0:1], scalar2=rstd_[0:NT],
                                                               op0=ALU.subtract, op1=ALU.mult)) if b % 2 == 0 else
                               (lambda e, b=b: e.activation(out=W[0:NT, b * 512:(b + 1) * 512], in_=PS[b][0:NT, :],
                                                            func=AF.Identity, scale=rstd_[0:NT], bias=nmr_[0:NT])),
                               reads=[sn, "ps%d" % b], writes=[wn + "_%d" % b])
                  if affine:
                      for b in range(4):
                          P.op("dve", lambda e, b=b: e.tensor_tensor(out=W[0:NT, b * 512:(b + 1) * 512], in0=W[0:NT, b * 512:(b + 1) * 512],
                                                                     in1=lng[0:NT, b * 512:(b + 1) * 512], op=ALU.mult),
                               reads=["lng"], writes=[wn + "_%d" % b])
                          P.op("pool" if b % 2 == 0 else "dve", lambda e, b=b: e.tensor_tensor(
                              out=W[0:NT, b * 512:(b + 1) * 512], in0=W[0:NT, b * 512:(b + 1) * 512],
                              in1=lnb[0:NT, b * 512:(b + 1) * 512], op=ALU.add),
                              reads=["lnb"], writes=[wn + "_%d" % b])

              LW1 = [(xt, "xt"), (lng, "lng"), (lnb, "lnb")]
              for j in range(TPG):
                  r0 = ck0 + j * NT
                  W, wn = LW1[j]
                  P.op("sp", lambda e, r0=r0, W=W: e.dma_start(out=W[0:NT, :], in_=xk[r0:r0 + NT, :]), writes=[wn], dma_slot=wn)
              for j in range(TPG):
                  W, wn = LW1[j]

                  def resid1(b, W=W, wn=wn):
                      P.op("dve", lambda e, b=b: e.scalar_tensor_tensor(
                          out=W[0:NT, b * 512:(b + 1) * 512], in0=W[0:NT, b * 512:(b + 1) * 512], scalar=ALPHA,
                          in1=PS[b][0:NT, :], op0=ALU.mult, op1=ALU.add), reads=["ps%d" % b], writes=[wn])
                  layer_norm_tile([fr(mixT[:, m, j * NT:(j + 1) * NT]) for m in range(16)], resid1, W, wn, j % 2, False)
                  for b in range(4):
                      def f(e, b=b, W=W):
                          last = None
                          for i in range(4):
                              m = 4 * b + i
                              last = e.transpose(PS[4 + (b % 2)][:, i * NT:(i + 1) * NT], W[0:NT, m * 128:(m + 1) * 128],
                                                 idn[0:NT, 0:NT])
                          return last
                      P.op("pe", f, reads=[wn, "idn"], writes=["ps%d" % (4 + b % 2)])
                      for i in range(4):
                          m = 4 * b + i
                          P.op("act", lambda e, b=b, i=i, m=m, j=j: e.activation(
                              out=cat[:, m, j * NT:(j + 1) * NT], in_=PS[4 + (b % 2)][:, i * NT:(i + 1) * NT],
                              func=AF.Identity, scale=ln1c[:, m:m + 1], bias=ln1c[:, 16 + m:17 + m]),
                              reads=["ps%d" % (4 + b % 2), "ln1c"], writes=["h1T"])
              if g == 0:
                  stop(8, [("h1T", fr(cat[:, :, :]), "h1T")])
              P.retire(["mixT", "qiT", "uT", "olat"] + ["xTe0", "xTe1", "xTe2", "xTe3"], ["gT%d" % i for i in range(NFF)])
              P.retire(["xt", "lng", "lnb", "tA", "tB", "scores0", "scores1", "nmT", "rd"], ["zc0", "zc1", "zc2", "zc3", "gtmp0", "gtmp1"])
              for i in range(NFF):
                  for half in range(2):
                      cidx = i + NFF * half
                      s, res = stream_w(wup[cidx, :, :], three=True)
                      pb = 4 + 2 * (i % 2) + half
                      zb = 2 * (i % 2) + half

                      def fu(e, s=s, pb=pb):
                          last = None
                          for k in range(16):
                              last = e.matmul(PS[pb][:, 0:G], s[:, k * 128:(k + 1) * 128], cat[:, k, :],
                                              start=(k == 0), stop=(k == 15))
                          return last
                      P.op("pe", fu, reads=[res, "h1T"], writes=["ps%d" % pb])
                      zr = "zc%d" % zb
                      P.op("act", lambda e, cidx=cidx, pb=pb, zb=zb: e.activation(
                          out=zc[:, zb, :], in_=PS[pb][:, 0:G], func=AF.Identity, scale=convw[:, cidx, 2:3], bias=convw[:, cidx, 3:4]),
                          reads=["ps%d" % pb, "convw"], writes=[zr])
                      P.op("dve", lambda e, cidx=cidx, pb=pb, zb=zb: e.scalar_tensor_tensor(
                          out=zc[:, zb, 1:G], in0=PS[pb][:, 0:G - 1], scalar=convw[:, cidx, 1:2], in1=zc[:, zb, 1:G],
                          op0=ALU.mult, op1=ALU.add), reads=["ps%d" % pb, "convw"], writes=[zr])
                      P.op("dve", lambda e, cidx=cidx, pb=pb, zb=zb: e.scalar_tensor_tensor(
                          out=zc[:, zb, 2:G], in0=PS[pb][:, 0:G - 2], scalar=convw[:, cidx, 0:1], in1=zc[:, zb, 2:G],
                          op0=ALU.mult, op1=ALU.add), reads=["ps%d" % pb, "convw"], writes=[zr])
                      if g > 0:
                          P.op("dve", lambda e, cidx=cidx, zb=zb: e.scalar_tensor_tensor(
                              out=zc[:, zb, 0:2], in0=halo[:, cidx, 0:2], scalar=convw[:, cidx, 0:1], in1=zc[:, zb, 0:2],
                              op0=ALU.mult, op1=ALU.add), reads=["halo%d" % cidx, "convw"], writes=[zr])
                          P.op("dve", lambda e, cidx=cidx, zb=zb: e.scalar_tensor_tensor(
                              out=zc[:, zb, 0:1], in0=halo[:, cidx, 1:2], scalar=convw[:, cidx, 1:2], in1=zc[:, zb, 0:1],
                              op0=ALU.mult, op1=ALU.add), reads=["halo%d" % cidx, "convw"], writes=[zr])
                      P.op("act", lambda e, cidx=cidx, pb=pb: e.activation(out=halo[:, cidx, :], in_=PS[pb][:, G - 2:G], func=AF.Copy),
                           reads=["ps%d" % pb], writes=["halo%d" % cidx])
                  za, zg = 2 * (i % 2), 2 * (i % 2) + 1
                  gb = i % 2
                  P.op("act", lambda e, za=za, gb=gb: e.activation(out=gtmp[:, gb, :], in_=zc[:, za, :], func=AF.Gelu_apprx_tanh),
                       reads=["zc%d" % za], writes=["gtmp%d" % gb])
                  P.op("dve", lambda e, i=i, zg=zg, gb=gb: e.tensor_tensor(out=gT[:, i, :], in0=gtmp[:, gb, :], in1=zc[:, zg, :], op=ALU.mult),
                       reads=["gtmp%d" % gb, "zc%d" % zg], writes=["gT%d" % i])
              if g == 0:
                  stop(9, [("gT", fr(gT[:, :, :]), "gT%d" % (NFF - 1))])
              for m in range(16):
                  pb = 4 + (m % 2)
                  for (k0, nkk) in ((0, 16), (16, 16), (32, 12)):
                      s, res = stream_w(wdn[m, :, k0 * 128:(k0 + nkk) * 128], nelem=nkk * 128, three=True)

                      def fd(e, s=s, pb=pb, k0=k0, nkk=nkk):
                          last = None
                          for k in range(nkk):
                              kk = k0 + k
                              last = e.matmul(PS[pb][:, 0:G], s[:, k * 128:(k + 1) * 128], gT[:, kk, :],
                                              start=(kk == 0), stop=(kk == NFF - 1))
                          return last
                      P.op("pe", fd, reads=[res] + ["gT%d" % (k0 + k) for k in range(nkk)], writes=["ps%d" % pb])
                  P.op("dve", lambda e, m=m, pb=pb: e.scalar_tensor_tensor(
                      out=cat[:, m, :], in0=fr(cat[:, m, :]), scalar=ALPHA, in1=PS[pb][:, 0:G], op0=ALU.mult, op1=ALU.add),
                      reads=["ps%d" % pb], writes=["h1T"])
              if g == 0:
                  stop(10, [("h2T", fr(cat[:, :, :]), "h1T")])
              P.retire(["zc0", "zc1", "zc2", "zc3", "gtmp0", "gtmp1"], ["xt", "lng", "lnb"])
              P.op("sp", lambda e: e.dma_start(out=lng[0:NT, :], in_=ln_d[2:3, :].partition_broadcast(NT)), writes=["lng"], dma_slot="lng")
              P.op("sp", lambda e: e.dma_start(out=lnb[0:NT, :], in_=ln_d[3:4, :].partition_broadcast(NT)), writes=["lnb"], dma_slot="lnb")
              LW2 = [(xt, "xt"), (xt, "xt")]
              for j in range(TPG):
                  J = g * TPG + j
                  W, wn = LW2[J % 2]

                  XB = [wn + "_%d" % b for b in range(4)]
                  P.retire([wn], XB)
                  layer_norm_tile([fr(cat[:, m, j * NT:(j + 1) * NT]) for m in range(16)], None, W, wn, J % 2, True)
                  if J == 0:
                      t = P.op("sp", lambda e, W=W: e.dma_start(out=out[0:NT - 2, :], in_=W[2:NT, :]), reads=XB, dma_slot="ost%d" % (J % 2))
                  else:
                      o0 = J * NT - 2
                      t = P.op("sp", lambda e, o0=o0, W=W: e.dma_start(out=out[o0:o0 + NT, :], in_=W[0:NT, :]), reads=XB,
                               dma_slot="ost%d" % (J % 2))
                  P.retire(XB, [wn])
                  out_toks.append(t)
              P.retire(["h1T"], ["cat"])
          P.final_wait("sp", out_toks[-2:])
          P.emit()
      except _Done:
        pass
    nc._dbg_dumps = dumps
    return nc


def t5_bucket_np(dist):
    max_exact = 16
    d_f = np.maximum(dist, 1).astype(np.float32)
    large = max_exact + (np.log(d_f / max_exact) / math.log(128 / max_exact) * (32 - max_exact)).astype(np.int32)
    large = np.minimum(large, 31)
    return np.where(dist < max_exact, dist, large)


_NC_CACHE = {}


def prep_shared(meta, rel_bias, w_in, kv_norm_g, w_uk, w_uv, w_pool, pool_scale, w_o, ln1_g, ln1_b,
                w_up, conv_w, conv_b, w_down, ln2_g, ln2_b):
    f = np.float32
    W = np.asarray(w_in[0], f)

    def blk(Wc):
        n = Wc.shape[1]
        return np.ascontiguousarray(Wc.reshape(16, 128, n).transpose(1, 0, 2)).reshape(128, 16 * n)
    cols = [C_QI + 128 * i for i in range(8)] + [C_U + 128 * i for i in range(8)] + [C_Q + 128 * i for i in range(8)]
    sh = {}
    sh["win_m"] = np.stack([blk(W[:, c:c + 128]) for c in cols])
    sh["win_k"] = blk(np.concatenate([W[:, C_KV:C_KV + 256], W[:, C_KI:C_KI + 64]], axis=1))
    sh["win_w"] = blk(W[:, C_WI:C_WI + 16])
    sh["wukT"] = np.ascontiguousarray(np.asarray(w_uk[0], f).transpose(2, 1, 0)).reshape(128, 8 * 256)
    sh["wuv"] = np.ascontiguousarray(np.asarray(w_uv[0], f).reshape(2, 128, 8, 128).transpose(1, 0, 2, 3)).reshape(128, 2048)
    sh["wpool"] = np.ascontiguousarray(np.asarray(w_pool[0], f).reshape(4, 2, 128, 256).transpose(2, 0, 1, 3)).reshape(128, 2048)
    Wo = np.asarray(w_o[0], f)
    sh["wo"] = np.stack([blk(Wo[:, 128 * m:128 * m + 128]) for m in range(16)])
    Wu = np.asarray(w_up[0], f)
    sh["wup"] = np.stack([blk(Wu[:, 128 * m:128 * m + 128]) for m in range(88)])
    Wd = np.asarray(w_down[0], f)
    wd = Wd.reshape(44, 128, 16, 128).transpose(2, 1, 0, 3)
    sh["wdn"] = np.ascontiguousarray(wd).reshape(16, 128, 44 * 128)
    cw = np.asarray(conv_w[0], f)
    cb = np.asarray(conv_b[0], f)
    cp = np.concatenate([cw, cb[None]], axis=0)
    sh["convp"] = np.ascontiguousarray(cp.reshape(4, 88, 128).transpose(2, 1, 0)).reshape(128, 88 * 4)
    sh["pscale"] = np.ascontiguousarray(np.asarray(pool_scale[0], f).reshape(8, 128).T)
    sh["gk"] = np.asarray(kv_norm_g[0], f).reshape(1, 256)
    sh["lnp"] = np.stack([np.asarray(a[0], f) for a in (ln1_g, ln1_b, ln2_g, ln2_b)])
    sh["ln1c"] = np.ascontiguousarray(np.concatenate([np.asarray(ln1_g[0], f).reshape(16, 128).T, np.asarray(ln1_b[0], f).reshape(16, 128).T], axis=1))
    sh["idn"] = np.eye(128, dtype=f)
    sh["ones"] = np.ones((128, 128), f)
    rb = np.asarray(rel_bias, f)
    p = np.arange(NT)[:, None]
    i = np.arange(NT)[None, :]
    dd = t5_bucket_np(np.maximum(i - p, 0))
    dp = t5_bucket_np(i - p + NT)
    sh["bd"] = np.ascontiguousarray(rb[dd].transpose(0, 2, 1)).reshape(NT, 8 * NT)
    sh["bp"] = np.ascontiguousarray(rb[dp].transpose(0, 2, 1)).reshape(NT, 8 * NT)
    sh["b31"] = np.ascontiguousarray(np.broadcast_to(rb[31][None, :, None], (NT, 8, NT))).reshape(NT, 8 * NT)
    sh["tri01"] = (i >= p).astype(f)
    sh["trineg"] = np.where(p >= i, 0.0, NEG).astype(f)
    return sh


def kernel(x, meta, rel_bias, w_in, kv_norm_g, w_uk, w_uv, w_pool, pool_scale, w_o, ln1_g, ln1_b,
           w_up, conv_w, conv_b, w_down, ln2_g, ln2_b):
    x = np.asarray(x, np.float32)
    meta = np.asarray(meta, np.float32)
    sh = prep_shared(meta, rel_bias, w_in, kv_norm_g, w_uk, w_uv, w_pool, pool_scale, w_o, ln1_g, ln1_b,
                     w_up, conv_w, conv_b, w_down, ln2_g, ln2_b)
    if "nc" not in _NC_CACHE:
        _NC_CACHE["nc"] = build_nc()
    nc = _NC_CACHE["nc"]
    in_maps = []
    for c in range(8):
        b, half = c // 2, c % 2
        base = 14 + 1024 * half
        hfull = np.concatenate([meta, x[b]], axis=0)
        xk = np.zeros((NK, D), np.float32)
        tok = base - KOFF + np.arange(NK)
        valid = tok >= 0
        xk[valid] = hfull[tok[valid]]
        m = dict(sh)
        m["xk"] = xk
        m["kflag"] = (~valid[:KOFF]).astype(np.uint8).reshape(1, KOFF)
        m["fixc"] = np.full((128, 1), 16.0 / 15.0 if half == 0 else 1.0, np.float32)
        in_maps.append(m)
    res = run_bass_kernel_spmd(nc, in_maps, core_ids=list(range(8)))
    outp = np.zeros((4, 2048, D), np.float32)
    for c in range(8):
        b, half = c // 2, c % 2
        outp[b, 1024 * half:1024 * (half + 1)] = res.results[c]["out"]
    return outp
```
